# Optimizing a Trainium2 kernel written in Bass

```python
import math
import jax, jax.numpy as jnp
from jax import lax
import numpy as np

D_MODEL = 1024
BATCH = 8
SEQ = 4096
DEPTH = 4
DEC_BATCH = 16
DEC_SEQ = 2048
PAST_LEN = 128

N_MIXERS = 3
D_FF = 2816
NORM_EPS = 1e-6

HG_HEADS = 8
HG_KDIM = D_MODEL // HG_HEADS
HG_VDIM = D_MODEL // HG_HEADS
HG_CHUNK = 64

SW_HEADS = 16
SW_KV_HEADS = 4
SW_HEAD_DIM = D_MODEL // SW_HEADS
SW_WINDOW = 128
SW_BLOCK = 128
ROPE_THETA = 500000.0
ROPE_DIMS = SW_HEAD_DIM // 4

GRID_W = 64
NA_HEADS = 16
NA_HEAD_DIM = D_MODEL // NA_HEADS
NA_MAX_ROWS = 8
NA_COLS = 16
NA_QCOLS = 16
NA_KSPAN = NA_QCOLS + NA_COLS

N_LAYERS_A = len(range(0, DEPTH, N_MIXERS))
N_LAYERS_B = len(range(1, DEPTH, N_MIXERS))
N_LAYERS_C = len(range(2, DEPTH, N_MIXERS))

kernel_name = "hybrid_hgrn2_swa_natten_macaron_encoder"

F32 = jnp.float32


def rmsnorm(x, g):
    xf = x.astype(F32)
    y = xf * lax.rsqrt(jnp.mean(xf * xf, axis=-1, keepdims=True) + NORM_EPS)
    return (y * g.astype(F32)).astype(x.dtype)


def swiglu_ffn(x, w1, w2):
    gate, up = jnp.split(x @ w1, 2, axis=-1)
    return (jax.nn.silu(gate) * up) @ w2


def gla_chunk_scan(q, k, v, logf):
    B, T, H, K = q.shape
    V = v.shape[-1]
    C = HG_CHUNK
    N = T // C

    def to_chunks(a):
        return jnp.moveaxis(a.reshape(B, N, C, H, a.shape[-1]), 1, 0)

    qc, kc, vc, fc = to_chunks(q), to_chunks(k), to_chunks(v), to_chunks(logf)
    incl = jnp.asarray(np.tril(np.ones((C, C), dtype=bool)))

    def step(S, inp):
        qi, ki, vi, fi = inp
        qf, kf, vf = qi.astype(F32), ki.astype(F32), vi.astype(F32)
        b = jnp.cumsum(fi.astype(F32), axis=1)
        o_inter = jnp.einsum('bchk,bhkv->bchv', qf * jnp.exp(b), S)
        rel = b[:, :, None] - b[:, None, :]
        rel = jnp.where(incl[None, :, :, None, None], rel, -jnp.inf)
        A = jnp.einsum('bthk,btshk,bshk->bhts', qf, jnp.exp(rel), kf)
        o_intra = jnp.einsum('bhts,bshv->bthv', A, vf)
        b_end = b[:, -1]
        k_dec = kf * jnp.exp(b_end[:, None] - b)
        S_new = jnp.exp(b_end)[..., None] * S + jnp.einsum('bchk,bchv->bhkv', k_dec, vf)
        return S_new, (o_inter + o_intra).astype(v.dtype)

    S0 = jnp.zeros((B, H, K, V), F32)
    _, o = lax.scan(step, S0, (qc, kc, vc, fc))
    return jnp.moveaxis(o, 0, 1).reshape(B, T, H, V)


def hgrn2_mixer(x, w_in, w_o, g_norm, lb):
    B, T, _ = x.shape
    H, K, V = HG_HEADS, HG_KDIM, HG_VDIM
    q, i, g, f_fw, f_bw = jnp.split(x @ w_in, 5, axis=-1)
    q = q.reshape(B, T, H, K)
    i = i.reshape(B, T, H, V)

    def gates(fpre, lbd):
        lbd = lbd.astype(F32)
        f = lbd + (1.0 - lbd) * jax.nn.sigmoid(fpre.astype(F32))
        logf = jnp.log(f).reshape(B, T, H, K)
        k = (1.0 - f).reshape(B, T, H, K).astype(x.dtype)
        return k, logf

    k_f, logf_f = gates(f_fw, lb[0])
    k_b, logf_b = gates(f_bw, lb[1])
    o_fwd = gla_chunk_scan(q, k_f, i, logf_f)
    rev = lambda a: jnp.flip(a, axis=1)
    o_bwd = rev(gla_chunk_scan(rev(q), rev(k_b), rev(i), rev(logf_b)))
    o = (o_fwd + o_bwd).astype(F32)
    o = o * lax.rsqrt(jnp.mean(o * o, axis=-1, keepdims=True) + NORM_EPS)
    o = o * g_norm.astype(F32).reshape(H, V)
    o = o.reshape(B, T, H * V).astype(x.dtype) * jax.nn.silu(g)
    return o @ w_o


def rope_partial(x):
    T = x.shape[1]
    half = ROPE_DIMS // 2
    inv = jnp.exp(-jnp.arange(half, dtype=F32) * (2.0 / ROPE_DIMS) * math.log(ROPE_THETA))
    ang = jnp.arange(T, dtype=F32)[:, None] * inv[None, :]
    cos = jnp.cos(ang)[None, :, None, :]
    sin = jnp.sin(ang)[None, :, None, :]
    xr = x[..., :ROPE_DIMS].astype(F32)
    x1, x2 = xr[..., :half], xr[..., half:]
    rot = jnp.concatenate([x1 * cos - x2 * sin, x2 * cos + x1 * sin], axis=-1)
    return jnp.concatenate([rot.astype(x.dtype), x[..., ROPE_DIMS:]], axis=-1)


def window_gqa_mixer(x, w_qkv, w_o, sinks):
    B, T, _ = x.shape
    Hq, Hk, d = SW_HEADS, SW_KV_HEADS, SW_HEAD_DIM
    G = Hq // Hk
    L = SW_BLOCK
    nb = T // L
    q, k, v = jnp.split(x @ w_qkv, [Hq * d, Hq * d + Hk * d], axis=-1)
    q = rope_partial(q.reshape(B, T, Hq, d))
    k = rope_partial(k.reshape(B, T, Hk, d))
    v = v.reshape(B, T, Hk, d)
    pad = ((0, 0), (L, L), (0, 0), (0, 0))
    kb = jnp.pad(k, pad).reshape(B, nb + 2, L, Hk, d)
    vb = jnp.pad(v, pad).reshape(B, nb + 2, L, Hk, d)
    kband = jnp.concatenate([kb[:, :-2], kb[:, 1:-1], kb[:, 2:]], axis=2)
    vband = jnp.concatenate([vb[:, :-2], vb[:, 1:-1], vb[:, 2:]], axis=2)
    qb = q.reshape(B, nb, L, Hk, G, d)
    s = jnp.einsum('bnqhgd,bnkhd->bnhgqk', qb, kband).astype(F32) * (d ** -0.5)
    qpos = np.arange(nb)[:, None, None] * L + np.arange(L)[None, :, None]
    kpos = (np.arange(nb)[:, None, None] - 1) * L + np.arange(3 * L)[None, None, :]
    mask = (np.abs(kpos - qpos) <= SW_WINDOW) & (kpos >= 0) & (kpos < T)
    s = jnp.where(jnp.asarray(mask)[None, :, None, None], s, -jnp.inf)
    sink = sinks.astype(F32).reshape(Hk, G)[None, None, :, :, None, None]
    m = jnp.maximum(jnp.max(s, axis=-1, keepdims=True), sink)
    p = jnp.exp(s - m)
    denom = jnp.sum(p, axis=-1, keepdims=True) + jnp.exp(sink - m)
    o = jnp.einsum('bnhgqk,bnkhd->bnqhgd', (p / denom).astype(v.dtype), vband)
    return o.reshape(B, T, Hq * d) @ w_o


def neighbourhood_mixer(x, w_qkv, w_o, rpb):
    B, T, _ = x.shape
    H, d = NA_HEADS, NA_HEAD_DIM
    rows = T // GRID_W
    kr = min(NA_MAX_ROWS, rows)
    q, k, v = jnp.split(x @ w_qkv, 3, axis=-1)
    qg = q.reshape(B, rows, GRID_W, H, d)
    kg = k.reshape(B, rows, GRID_W, H, d)
    vg = v.reshape(B, rows, GRID_W, H, d)
    nqb = GRID_W // NA_QCOLS
    col_start = np.clip(np.arange(nqb) * NA_QCOLS - NA_COLS // 2, 0, GRID_W - NA_KSPAN)
    col_idx = col_start[:, None] + np.arange(NA_KSPAN)[None, :]
    qcol = np.arange(nqb)[:, None] * NA_QCOLS + np.arange(NA_QCOLS)[None, :]
    cs = np.clip(qcol - NA_COLS // 2, 0, GRID_W - NA_COLS)
    kcol = col_idx[:, None, :]
    col_valid = (kcol >= cs[..., None]) & (kcol < cs[..., None] + NA_COLS)
    dc = np.clip(kcol - qcol[..., None], -(NA_COLS - 1), NA_COLS - 1) + NA_COLS - 1
    bias_c = rpb.astype(F32)[:, :, dc]
    bias_c = jnp.where(jnp.asarray(col_valid)[None, None], bias_c, -jnp.inf)
    scale = d ** -0.5

    def row_block(r):
        rs = jnp.clip(r - kr // 2, 0, rows - kr)
        k_rows = lax.dynamic_slice_in_dim(kg, rs, kr, axis=1)
        v_rows = lax.dynamic_slice_in_dim(vg, rs, kr, axis=1)
        k_blk = k_rows[:, :, col_idx]
        v_blk = v_rows[:, :, col_idx]
        q_row = lax.dynamic_index_in_dim(qg, r, axis=1, keepdims=False)
        q_row = q_row.reshape(B, nqb, NA_QCOLS, H, d)
        s = jnp.einsum('bjqhd,bajchd->bhjqac', q_row, k_blk).astype(F32) * scale
        dr = rs + jnp.arange(kr) - r + NA_MAX_ROWS - 1
        bias = jnp.transpose(bias_c[:, dr], (0, 2, 3, 1, 4))
        s = s + bias[None]
        shp = s.shape
        p = jax.nn.softmax(s.reshape(shp[:4] + (kr * NA_KSPAN,)), axis=-1).reshape(shp)
        o = jnp.einsum('bhjqac,bajchd->bjqhd', p.astype(v.dtype), v_blk)
        return o.reshape(B, GRID_W, H * d)

    o = lax.map(row_block, jnp.arange(rows))
    o = jnp.moveaxis(o, 0, 1).reshape(B, T, H * d)
    return o @ w_o


def run_trunk(x, norm_g, final_norm_g, ffn_w1, ffn_w2, hg_w_in, hg_w_o, hg_g_norm, lb_all,
              sw_w_qkv, sw_w_o, sw_sinks, na_w_qkv, na_w_o, na_rpb):
    ia = ib = ic = 0
    for li in range(DEPTH):
        x = x + 0.5 * swiglu_ffn(rmsnorm(x, norm_g[li, 0]), ffn_w1[li, 0], ffn_w2[li, 0])
        h = rmsnorm(x, norm_g[li, 1])
        kind = li % N_MIXERS
        if kind == 0:
            mix = hgrn2_mixer(h, hg_w_in[ia], hg_w_o[ia], hg_g_norm[ia], lb_all[:, li])
            ia += 1
        elif kind == 1:
            mix = window_gqa_mixer(h, sw_w_qkv[ib], sw_w_o[ib], sw_sinks[ib])
            ib += 1
        else:
            mix = neighbourhood_mixer(h, na_w_qkv[ic], na_w_o[ic], na_rpb[ic])
            ic += 1
        x = x + mix
        x = x + 0.5 * swiglu_ffn(rmsnorm(x, norm_g[li, 2]), ffn_w1[li, 1], ffn_w2[li, 1])
    return rmsnorm(x, final_norm_g)


def setup_inputs(seed: int = 0) -> dict:
    key = jax.random.key(seed)
    ks = jax.random.split(key, 20)
    D = D_MODEL
    nrm = lambda k, shape, s: jax.random.normal(k, shape, F32) * s
    return {
        "x_prompt": nrm(ks[0], (BATCH, SEQ, D), 1.0),
        "x_sample": nrm(ks[1], (DEC_BATCH, DEC_SEQ, D), 1.0),
        "norm_g": 1.0 + nrm(ks[2], (DEPTH, 3, D), 0.02),
        "final_norm_g": 1.0 + nrm(ks[3], (D,), 0.02),
        "ffn_w1": nrm(ks[4], (DEPTH, 2, D, 2 * D_FF), D ** -0.5),
        "ffn_w2": nrm(ks[5], (DEPTH, 2, D_FF, D), D_FF ** -0.5),
        "hg_w_in": nrm(ks[6], (N_LAYERS_A, D, 5 * D), D ** -0.5),
        "hg_w_o": nrm(ks[7], (N_LAYERS_A, HG_HEADS * HG_VDIM, D), (HG_HEADS * HG_VDIM) ** -0.5),
        "hg_g_norm": 1.0 + nrm(ks[8], (N_LAYERS_A, HG_HEADS * HG_VDIM), 0.02),
        "hg_lb": nrm(ks[9], (2, DEPTH, HG_HEADS * HG_KDIM), 0.5),
        "sw_w_qkv": nrm(ks[10], (N_LAYERS_B, D, (SW_HEADS + 2 * SW_KV_HEADS) * SW_HEAD_DIM), D ** -0.5),
        "sw_w_o": nrm(ks[11], (N_LAYERS_B, SW_HEADS * SW_HEAD_DIM, D), (SW_HEADS * SW_HEAD_DIM) ** -0.5),
        "sw_sinks": nrm(ks[12], (N_LAYERS_B, SW_HEADS), 0.5),
        "na_w_qkv": nrm(ks[13], (N_LAYERS_C, D, 3 * NA_HEADS * NA_HEAD_DIM), D ** -0.5),
        "na_w_o": nrm(ks[14], (N_LAYERS_C, NA_HEADS * NA_HEAD_DIM, D), (NA_HEADS * NA_HEAD_DIM) ** -0.5),
        "na_rpb": nrm(ks[15], (N_LAYERS_C, NA_HEADS, 2 * NA_MAX_ROWS - 1, 2 * NA_COLS - 1), 0.1),
    }


def reference(x_prompt, x_sample, norm_g, final_norm_g, ffn_w1, ffn_w2, hg_w_in, hg_w_o,
              hg_g_norm, hg_lb, sw_w_qkv, sw_w_o, sw_sinks, na_w_qkv, na_w_o, na_rpb):
    lb_sm = jax.nn.softmax(hg_lb.astype(F32), axis=1)
    lb_all = jnp.cumsum(lb_sm, axis=1) - lb_sm[:, :1]
    y_prompt = run_trunk(x_prompt, norm_g, final_norm_g, ffn_w1, ffn_w2, hg_w_in, hg_w_o,
                         hg_g_norm, lb_all, sw_w_qkv, sw_w_o, sw_sinks, na_w_qkv, na_w_o, na_rpb)
    y_sample = run_trunk(x_sample, norm_g, final_norm_g, ffn_w1, ffn_w2, hg_w_in, hg_w_o,
                         hg_g_norm, lb_all, sw_w_qkv, sw_w_o, sw_sinks, na_w_qkv, na_w_o, na_rpb)
    return (y_prompt, y_sample)
```

```python
import numpy as np
import ml_dtypes
import concourse.bass as bass
import concourse.mybir as mybir
from concourse.bass_utils import run_bass_kernel_spmd

F32 = mybir.dt.float32
BF16 = mybir.dt.bfloat16
AF = mybir.ActivationFunctionType
ALU = mybir.AluOpType
AX = mybir.AxisListType

D = 1024
DFF = 2816
EPS = 1e-6
NCORES = 8
P = 128
ARENA_BYTES = 204 * 1024
NEG = -30000.0


class Op:
    __slots__ = ("eng", "fn", "cdeps", "ddeps", "dma", "val", "has_dep", "seq", "idx", "sem")


COMPUTE = ("pe", "act", "dve", "pool")
ENGS = ("sp", "act", "dve", "pool", "pe")


class Sched:
    def __init__(self, nc):
        self.nc = nc
        self.q = {e: [] for e in ENGS}
        self.lastw = {}
        self.rd_c = {}
        self.rd_d = {}
        self.chan_cnt = {}
        self.chan_sem = {}
        self.chan_last = {}
        self.pending_barrier = {e: None for e in ENGS}
        self.pkeys = set(["G0", "G1", "U0", "U1", "O0", "O1", "pt0", "pt1", "PJ0", "PJ1", "PJ2", "PJ3",
                          "ST0", "ST1", "OT0", "OT1", "OT", "PA0", "PA1", "PU0", "PU1", "POB0", "POB1", "PSS0", "PSS1"])
        self.epoch_sems = None
        self.n_ops = 0
        self.new_epoch()

    def new_epoch(self):
        self.epoch_sems = {e: self.nc.alloc_semaphore("ep%d_%s" % (self.n_ops, e)) for e in COMPUTE}

    def _add_dep(self, o, d):
        if d is None or d is o:
            return
        if d.dma is not None:
            o.ddeps.add(d)
        else:
            cur = o.cdeps.get(d.eng)
            if cur is None or cur.idx < d.idx:
                o.cdeps[d.eng] = d

    def op(self, eng, fn, reads=(), writes=(), dma=None):
        o = Op()
        o.eng = eng
        o.fn = fn
        o.cdeps = {}
        o.ddeps = set()
        o.dma = dma
        o.has_dep = False
        o.seq = None
        o.val = None
        o.idx = len(self.q[eng])
        o.sem = None
        self.n_ops += 1
        pb = self.pending_barrier[eng]
        if pb is not None:
            for d in pb:
                self._add_dep(o, d)
            self.pending_barrier[eng] = None
        for r in reads:
            self._add_dep(o, self.lastw.get(r))
            if r in self.pkeys:
                for e2, d in self.rd_c.get(r, {}).items():
                    if e2 != eng:
                        self._add_dep(o, d)
        for w in writes:
            self._add_dep(o, self.lastw.get(w))
            for d in self.rd_c.get(w, {}).values():
                self._add_dep(o, d)
            for d in self.rd_d.get(w, ()):
                self._add_dep(o, d)
        for r in reads:
            if dma is not None:
                self.rd_d.setdefault(r, []).append(o)
            else:
                self.rd_c.setdefault(r, {})[eng] = o
        for w in writes:
            self.lastw[w] = o
            self.rd_c[w] = {}
            self.rd_d[w] = []
        if dma is not None:
            self._add_dep(o, self.chan_last.get(dma))
            c = self.chan_cnt.get(dma, 0) + 16
            self.chan_cnt[dma] = c
            o.val = c
            if dma not in self.chan_sem:
                self.chan_sem[dma] = self.nc.alloc_semaphore("ch_" + str(dma))
            o.sem = self.chan_sem[dma]
            self.chan_last[dma] = o
        else:
            o.sem = self.epoch_sems[eng]
        self.q[eng].append(o)
        return o

    def barrier(self):
        deps = []
        for e in COMPUTE:
            for o in reversed(self.q[e]):
                if o.dma is None:
                    deps.append(o)
                    break
        deps.extend(self.chan_last.values())
        for e in ENGS:
            old = self.pending_barrier[e]
            self.pending_barrier[e] = list(deps) + (list(old) if old else [])

    def emit(self, final_waits=True):
        nc = self.nc
        for e in ENGS:
            for o in self.q[e]:
                for d in o.cdeps.values():
                    d.has_dep = True
        cnt = {}
        for e in ENGS:
            for o in self.q[e]:
                if o.dma is None and o.has_dep:
                    k = o.sem.num
                    cnt[k] = cnt.get(k, 0) + 1
                    o.seq = cnt[k]
        sched = self

        def run_engine(ename, eng):
            waited = {}
            for o in sched.q[ename]:
                for d in o.cdeps.values():
                    if d.eng == ename and ename == "pe":
                        continue
                    if waited.get(d.sem.num, 0) < d.seq:
                        eng.wait_ge(d.sem, d.seq)
                        waited[d.sem.num] = d.seq
                for d in o.ddeps:
                    if waited.get(d.sem.num, 0) < d.val:
                        eng.wait_ge(d.sem, d.val)
                        waited[d.sem.num] = d.val
                ins = o.fn(eng)
                if o.dma is not None:
                    ins.then_inc(o.sem, 16)
                elif o.has_dep:
                    ins.then_inc(o.sem, 1)
            if ename == "sp" and final_waits:
                for ch, o in sched.chan_last.items():
                    if waited.get(o.sem.num, 0) < o.val:
                        eng.wait_ge(o.sem, o.val)

        with nc.Block() as block:
            @block.sync
            def _(eng):
                run_engine("sp", eng)

            @block.scalar
            def _(eng):
                run_engine("act", eng)

            @block.vector
            def _(eng):
                run_engine("dve", eng)

            @block.gpsimd
            def _(eng):
                run_engine("pool", eng)

            @block.tensor
            def _(eng):
                run_engine("pe", eng)


class Ring:
    def __init__(self, name, n):
        self.name = name
        self.n = n

    def key(self, i):
        return "%s#%d" % (self.name, i % self.n)

    def slot(self, i):
        return i % self.n


class Builder:
    def __init__(self, seqs, n_layers=4, dbg=None):
        self.seqs = list(seqs)
        self.T = sum(seqs)
        self.NB = self.T // P
        self.n_layers = n_layers
        self.dbg = dbg or {}
        nc = bass.Bass("TRN2", target_bir_lowering=False)
        self.nc = nc
        self.S = Sched(nc)
        T = self.T

        def din(name, shape, dt=F32):
            return nc.dram_tensor(name, list(shape), dt, kind="ExternalInput").ap()

        self.x = din("x", [T, D])
        self.norm_g = din("norm_g", [4, 3, D])
        self.final_norm_g = din("final_norm_g", [1, D])
        self.ffn_w1 = din("ffn_w1", [4, 2, D, 2 * DFF])
        self.ffn_w2 = din("ffn_w2", [4, 2, DFF, D])
        self.hg_w_in = din("hg_w_in", [2, D, 5 * D])
        self.hg_w_o = din("hg_w_o", [2, D, D])
        self.hg_g_norm = din("hg_g_norm", [2, D])
        self.hg_lb = din("hg_lb", [2, 4, D])
        self.sw_w_qkv = din("sw_w_qkv", [1, D, 1536])
        self.sw_w_o = din("sw_w_o", [1, D, D])
        self.sw_sinks = din("sw_sinks", [1, 16])
        self.na_w_qkv = din("na_w_qkv", [1, D, 3072])
        self.na_w_o = din("na_w_o", [1, D, D])
        self.na_bias = din("na_bias", [16, 15, 64, 64])
        self.consts = din("consts", [P, 2048])
        self.y = nc.dram_tensor("y", [T, D], F32, kind="ExternalOutput").ap()

        self.psum = nc.alloc_psum_tensor("psum", [P, 4096], F32).ap()
        self.bank = [self.psum[:, i * 512:(i + 1) * 512] for i in range(8)]
        self.ps = [self.bank[0], self.bank[1], self.bank[2], self.bank[3], self.bank[6], self.bank[7]]
        self.pt = [self.bank[4].bitcast(BF16), self.bank[5].bitcast(BF16)]
        self.rope = din("rope", [4096, 16])
        self.qkv_scr = nc.dram_tensor("qkv_scr", [T, 3072], BF16, kind="Internal").ap()
        self.hg_scr = nc.dram_tensor("hg_scr", [T // P, P, 8192], BF16, kind="Internal").ap()
        self.ar_scr = nc.dram_tensor("ar_scr", [T // P, P, 64], F32, kind="Internal").ap()
        self.obw_scr = nc.dram_tensor("obw_scr", [T // P, P, D], F32, kind="Internal").ap()

        self.arena = nc.alloc_sbuf_tensor("arena", [P, ARENA_BYTES // 2], BF16).ap()
        self.a_off = 0
        self.a_mark = 0
        self.ident = self.alloc([P], BF16)
        self.identf = self.alloc([P], F32)
        self.neghalf = self.alloc([4], F32)
        self.maskL = self.alloc([P], BF16)
        self.maskR = self.alloc([P], BF16)
        self.cstage = self.alloc([512], F32)
        self.a_mark = self.a_off
        self.S.op("sp", lambda e: e.dma_start(out=self.cstage, in_=self.consts[:, 128:640]),
                  writes=["cstage"], dma="cst")
        self.S.op("dve", lambda e: e.tensor_copy(out=self.maskL, in_=self.cstage[:, 0:128]),
                  reads=["cstage"], writes=["maskL"])
        self.S.op("dve", lambda e: e.tensor_copy(out=self.maskR, in_=self.cstage[:, 128:256]),
                  reads=["cstage"], writes=["maskR"])
        self.S.op("pool", lambda e: e.memset(self.neghalf, -0.5), writes=["neghalf"])
        S = self.S
        S.op("sp", lambda e: e.dma_start(out=self.identf, in_=self.consts[:, 0:128]),
             writes=["identf"], dma="cst")
        S.op("dve", lambda e: e.tensor_copy(out=self.ident, in_=self.identf),
             reads=["identf"], writes=["ident"])

    def alloc(self, fshape, dt):
        n = 1
        for v in fshape:
            n *= v
        nbytes = n * (4 if dt == F32 else 2)
        nbytes = (nbytes + 63) // 64 * 64
        assert self.a_off + nbytes <= ARENA_BYTES, ("SBUF arena overflow", self.a_off, nbytes)
        v = self.arena[:, self.a_off // 2:(self.a_off + nbytes) // 2]
        self.a_off += nbytes
        if dt == F32:
            v = v.bitcast(F32)
        v = v[:, 0:n]
        if len(fshape) == 2:
            v = v.rearrange("p (a b) -> p a b", a=fshape[0])
        elif len(fshape) == 3:
            v = v.rearrange("p (a b c) -> p a b c", a=fshape[0], b=fshape[1])
        return v

    def free_stage(self):
        self.a_off = self.a_mark

    def load_weight(self, sb, dram2d, KC, key, chan):
        S = self.S
        for kc in range(KC):
            S.op("pool", (lambda e, kc=kc: e.dma_start(out=sb[:, kc, :], in_=dram2d[kc * P:(kc + 1) * P, :],
                                                       max_dma_last_dim=8192)),
                 writes=[key], dma=chan)

    def load_bcast(self, sb, dram_row, key, chan="cst"):
        n = dram_row.shape[-1]
        self.S.op("sp", lambda e: e.dma_start(out=sb, in_=dram_row.broadcast_to([P, n])),
                  writes=[key], dma=chan)

    def rsqrt_eps(self, ss, kss):
        S = self.S
        S.op("pool", lambda e: e.tensor_scalar(out=ss[:, 2:3], in0=ss[:, 0:1], scalar1=EPS, scalar2=1.0,
                                               op0=ALU.add, op1=ALU.mult),
             reads=[kss], writes=[kss + "e"])
        S.op("pool", lambda e: e.tensor_tensor(out=ss[:, 1:2], in0=ss[:, 2:3], in1=self.neghalf[:, 0:1], op=ALU.pow),
             reads=[kss + "e", "neghalf"], writes=[kss + "r"])

    def norm_T(self, xt, kx, gb, kg, h, kh, junk, kj, ss, kss, ptile, kpt, hT, khT, evac_eng="act", tr=True):
        S = self.S
        S.op("act", lambda e: e.activation(out=junk, in_=xt, func=AF.Square, scale=float(D ** -0.5),
                                           accum_out=ss[:, 0:1]),
             reads=[kx], writes=[kj, kss])
        self.rsqrt_eps(ss, kss)
        S.op("dve", lambda e: e.scalar_tensor_tensor(out=h, in0=xt, scalar=ss[:, 1:2], in1=gb,
                                                     op0=ALU.mult, op1=ALU.mult),
             reads=[kx, kss + "r", kg], writes=[kh])
        if not tr:
            return
        self.norm_tr(h, kh, ptile, kpt, hT, khT, evac_eng)

    def norm_tr(self, h, kh, ptile, kpt, hT, khT, evac_eng="act"):
        S = self.S
        for kc in range(8):
            S.op("pe", (lambda e, kc=kc: e.transpose(out=ptile[:, kc * P:(kc + 1) * P],
                                                     in_=h[:, kc * P:(kc + 1) * P], identity=self.ident)),
                 reads=[kh, "ident"], writes=[kpt])
        hT2 = hT.rearrange("p a b -> p (a b)")
        if evac_eng == "act":
            S.op("act", lambda e: e.copy(out=hT2, in_=ptile), reads=[kpt], writes=[khT])
        else:
            S.op("dve", lambda e: e.tensor_copy(out=hT2, in_=ptile), reads=[kpt], writes=[khT])

    def ffn_stage(self, li, j, src, final_norm=False):
        nc, S = self.nc, self.S
        NB = self.NB
        tag = "f%d%d_" % (li, j)
        w1 = self.alloc([8, 2 * DFF], BF16)
        w2 = self.alloc([22, D], BF16)
        gb = self.alloc([D], F32)
        xin = [self.alloc([D], F32) for i in range(3)]
        xo = [self.alloc([D], F32) for i in range(2)]
        h = [self.alloc([D], BF16) for i in range(2)]
        hT = [self.alloc([8, P], BF16) for i in range(2)]
        a = [self.alloc([DFF], BF16) for i in range(2)]
        aT = [self.alloc([22, P], BF16) for i in range(2)]
        sg = [self.alloc([512], F32) for i in range(2)]
        junk = self.alloc([D], BF16)
        ss = [self.alloc([4], F32) for i in range(2)]
        if final_norm:
            gf = self.alloc([D], F32)
            self.load_bcast(gf, self.final_norm_g, "gf")
            ss2 = [self.alloc([4], F32) for i in range(2)]

        self.load_weight(w1, self.ffn_w1[li, j], 8, "w1", "w1")
        self.load_weight(w2, self.ffn_w2[li, j], 22, "w2", "w2")
        self.load_bcast(gb, self.norm_g[li, 2 * j:2 * j + 1, :], "gb")

        slices = [(c, min(512, DFF - c)) for c in range(0, DFF, 512)]
        G = [self.ps[0], self.ps[1]]
        U = [self.ps[2], self.ps[3]]
        O = [self.ps[4], self.ps[5]]

        def load_x(b):
            if b < NB:
                s = b % 3
                S.op("sp", lambda e: e.dma_start(out=xin[s], in_=src[b * P:(b + 1) * P, :]),
                     writes=["xin%d" % s], dma="xin%d" % s)

        def do_norm(b, part=2):
            if b < NB:
                s3, s2 = b % 3, b % 2
                if part in (0, 2):
                    self.norm_T(xin[s3], "xin%d" % s3, gb, "gb", h[s2], "h%d" % s2, junk, "junk",
                                ss[s2], "ss%d" % s2, self.pt[0], "pt0", hT[s2], "hT%d" % s2, tr=(part == 2))
                if part == 1:
                    self.norm_tr(h[s2], "h%d" % s2, self.pt[0], "pt0", hT[s2], "hT%d" % s2)

        load_x(0)
        load_x(1)
        do_norm(0)
        def do_block(b):
            s3, s2 = b % 3, b % 2
            load_x(b + 2)
            akeys = ["a%d_%d" % (s2, si) for si in range(len(slices))]

            def tr_round(r):
                k0 = r * 8
                k1 = min(22, k0 + 8)
                need = sorted(set(akeys[(kc * P) // 512] for kc in range(k0, k1)))
                for kc in range(k0, k1):
                    S.op("pe", (lambda e, kc=kc, k0=k0: e.transpose(
                        out=self.pt[1][:, (kc - k0) * P:(kc - k0 + 1) * P],
                        in_=a[s2][:, kc * P:(kc + 1) * P], identity=self.ident)),
                        reads=need + ["ident"], writes=["pt1"])
                n = k1 - k0
                dst = aT[s2][:, k0:k1, :].rearrange("p a b -> p (a b)")
                eng = "act" if r != 1 else "dve"
                if eng == "act":
                    S.op("act", lambda e: e.copy(out=dst, in_=self.pt[1][:, 0:n * P]),
                         reads=["pt1"], writes=["aT%d_%d" % (s2, r)])
                else:
                    S.op("dve", lambda e: e.tensor_copy(out=dst, in_=self.pt[1][:, 0:n * P]),
                         reads=["pt1"], writes=["aT%d_%d" % (s2, r)])

            for si, (c0, w) in enumerate(slices):
                g = si % 2
                for kc in range(8):
                    S.op("pe", (lambda e, kc=kc, c0=c0, w=w, g=g: e.matmul(
                        out=G[g][:, 0:w], lhsT=hT[s2][:, kc, :], rhs=w1[:, kc, c0:c0 + w],
                        start=(kc == 0), stop=(kc == 7))),
                        reads=["hT%d" % s2, "w1"], writes=["G%d" % g])
                for kc in range(8):
                    S.op("pe", (lambda e, kc=kc, c0=c0, w=w, g=g: e.matmul(
                        out=U[g][:, 0:w], lhsT=hT[s2][:, kc, :], rhs=w1[:, kc, DFF + c0:DFF + c0 + w],
                        start=(kc == 0), stop=(kc == 7))),
                        reads=["hT%d" % s2, "w1"], writes=["U%d" % g])
                S.op("act", (lambda e, w=w, g=g: e.activation(out=sg[g][:, 0:w], in_=G[g][:, 0:w], func=AF.Silu)),
                     reads=["G%d" % g], writes=["sg%d" % g])
                S.op("dve", (lambda e, c0=c0, w=w, g=g: e.tensor_tensor(
                    out=a[s2][:, c0:c0 + w], in0=sg[g][:, 0:w], in1=U[g][:, 0:w], op=ALU.mult)),
                    reads=["sg%d" % g, "U%d" % g], writes=["a%d_%d" % (s2, si)])
                if si == 0:
                    do_norm(b + 1, 0)
                if si == 3:
                    do_norm(b + 1, 1)
                    tr_round(0)

            def phase_b(r):
                k0 = r * 8
                k1 = min(22, k0 + 8)
                for kc in range(k0, k1):
                    for hf in range(2):
                        S.op("pe", (lambda e, kc=kc, hf=hf: e.matmul(
                            out=O[hf], lhsT=aT[s2][:, kc, :], rhs=w2[:, kc, hf * 512:(hf + 1) * 512],
                            start=(kc == 0), stop=(kc == 21))),
                            reads=["aT%d_%d" % (s2, r), "w2"], writes=["O%d" % hf])

            tr_round(1)
            phase_b(0)
            tr_round(2)
            phase_b(1)
            phase_b(2)
            for hf in range(2):
                S.op("dve", (lambda e, hf=hf: e.scalar_tensor_tensor(
                    out=xo[s2][:, hf * 512:(hf + 1) * 512], in0=O[hf], scalar=0.5,
                    in1=xin[s3][:, hf * 512:(hf + 1) * 512], op0=ALU.mult, op1=ALU.add)),
                    reads=["O%d" % hf, "xin%d" % s3], writes=["xo%d_%d" % (s2, hf)])
            okeys = ["xo%d_0" % s2, "xo%d_1" % s2]
            if final_norm:
                S.op("act", lambda e: e.activation(out=junk, in_=xo[s2], func=AF.Square, scale=float(D ** -0.5),
                                                   accum_out=ss2[s2][:, 0:1]),
                     reads=okeys, writes=["junk", "ssf%d" % s2])
                self.rsqrt_eps(ss2[s2], "ssf%d" % s2)
                S.op("dve", lambda e: e.scalar_tensor_tensor(out=xo[s2], in0=xo[s2], scalar=ss2[s2][:, 1:2], in1=gf,
                                                             op0=ALU.mult, op1=ALU.mult),
                     reads=okeys + ["ssf%dr" % s2, "gf"], writes=okeys)
            S.op("sp", lambda e: e.dma_start(out=self.y[b * P:(b + 1) * P, :], in_=xo[s2]),
                 reads=okeys, dma="xo%d" % s2)

        for b in range(NB):
            do_block(b)
        S.barrier()
        self.free_stage()


    def proj_stage(self, li, w_dram, N, evac, post, extra_setup=None, finish=None, nx=3):
        nc, S = self.nc, self.S
        NB = self.NB
        w = self.alloc([8, N], BF16)
        gb = self.alloc([D], F32)
        xin = [self.alloc([D], F32) for i in range(nx)]
        h = [self.alloc([D], BF16) for i in range(2)]
        hT = [self.alloc([8, P], BF16) for i in range(2)]
        junk = self.alloc([D], BF16)
        ss = [self.alloc([4], F32) for i in range(2)]
        self.load_weight(w, w_dram, 8, "wp", "w1")
        self.load_bcast(gb, self.norm_g[li, 1:2, :], "gb")
        if extra_setup is not None:
            extra_setup()
        slices = [(c, min(512, N - c)) for c in range(0, N, 512)]
        src = self.y

        def load_x(b):
            if b < NB:
                s_ = b % nx
                S.op("sp", lambda e: e.dma_start(out=xin[s_], in_=src[b * P:(b + 1) * P, :]),
                     writes=["xin%d" % s_], dma="xin%d" % s_)

        def do_norm(b, part=2):
            if b < NB:
                s3, s2 = b % nx, b % 2
                if part in (0, 2):
                    self.norm_T(xin[s3], "xin%d" % s3, gb, "gb", h[s2], "h%d" % s2, junk, "junk",
                                ss[s2], "ss%d" % s2, self.pt[0], "pt0", hT[s2], "hT%d" % s2, tr=(part == 2))
                if part == 1:
                    self.norm_tr(h[s2], "h%d" % s2, self.pt[0], "pt0", hT[s2], "hT%d" % s2)

        def do_block(b):
            s2 = b % 2
            load_x(b + nx - 1)
            for si, (c0, wd) in enumerate(slices):
                pb = si % 4
                for kc in range(8):
                    S.op("pe", (lambda e, kc=kc, pb=pb, c0=c0, wd=wd: e.matmul(
                        out=self.ps[pb][:, 0:wd], lhsT=hT[s2][:, kc, :], rhs=w[:, kc, c0:c0 + wd],
                        start=(kc == 0), stop=(kc == 7))),
                        reads=["hT%d" % s2, "wp"], writes=["PJ%d" % pb])
                evac(b, si, self.ps[pb][:, 0:wd], "PJ%d" % pb)
                if si == 0:
                    do_norm(b + 1, 0)
                if si == len(slices) - 1:
                    do_norm(b + 1, 1)
            post(b)

        for b_ in range(nx - 1):
            load_x(b_)
        do_norm(0)
        for b in range(NB):
            do_block(b)
        if finish is not None:
            finish()
        S.barrier()
        self.free_stage()

    def seq_blocks(self):
        b0 = 0
        for si, L in enumerate(self.seqs):
            yield si, b0, L // P
            b0 += L // P

    def swa_proj(self, li):
        S = self.S
        st = [self.alloc([1792], BF16) for i in range(2)]
        cs = [self.alloc([16], F32) for i in range(2)]
        tmp = [self.alloc([8, 8], F32) for i in range(4)]
        pos = []
        for L in self.seqs:
            pos.extend(range(0, L, P))

        def evac(b, si, ps, kps):
            s2 = b % 2
            if si == 0:
                p0 = pos[b]
                S.op("sp", lambda e: e.dma_start(out=cs[s2], in_=self.rope[p0:p0 + P, :]),
                     writes=["cs%d" % s2], dma="cs%d" % s2)
            nh = 8 if si < 2 else 4
            if si < 2:
                dst = st[s2][:, si * 512:(si + 1) * 512]
                S.op("act", lambda e: e.copy(out=dst, in_=ps), reads=[kps], writes=["st%d_%d" % (s2, si)])
                dv = dst.rearrange("p (h d) -> p h d", h=8)
            else:
                kd = st[s2][:, 1024:1536].rearrange("p (h t d) -> p h t d", h=4, t=2)
                S.op("act", lambda e: e.copy(out=kd[:, :, 0, :], in_=ps[:, 0:256].rearrange("p (h d) -> p h d", h=4)),
                     reads=[kps], writes=["st%d_%d" % (s2, si)])
                S.op("act", lambda e: e.copy(out=st[s2][:, 1536:1792], in_=ps[:, 256:512]),
                     reads=[kps], writes=["st%d_v" % s2])
                dv = kd[:, :, 0, :]
            import os
            dbg = int(os.environ.get("SWA_DBG", "0"))
            if dbg & 1:
                return
            pv = ps[:, 0:nh * 64].rearrange("p (h d) -> p h d", h=nh)
            x1, x2 = pv[:, :, 0:8], pv[:, :, 8:16]
            cosb = cs[s2][:, 0:8].unsqueeze(1).broadcast_to([P, nh, 8])
            sinb = cs[s2][:, 8:16].unsqueeze(1).broadcast_to([P, nh, 8])
            t = [tt[:, 0:nh, :] for tt in tmp]
            kt = ["rt0", "rt1", "rt2", "rt3"]
            S.op("dve", lambda e: e.tensor_tensor(out=t[0], in0=x1, in1=cosb, op=ALU.mult),
                 reads=[kps, "cs%d" % s2, "st%d_%d" % (s2, si)], writes=[kt[0]])
            S.op("dve", lambda e: e.tensor_tensor(out=t[1], in0=x2, in1=sinb, op=ALU.mult),
                 reads=[kps, "cs%d" % s2, "st%d_%d" % (s2, si)], writes=[kt[1]])
            S.op("dve", lambda e: e.tensor_tensor(out=t[2], in0=x2, in1=cosb, op=ALU.mult),
                 reads=[kps, "cs%d" % s2, "st%d_%d" % (s2, si)], writes=[kt[2]])
            S.op("dve", lambda e: e.tensor_tensor(out=t[3], in0=x1, in1=sinb, op=ALU.mult),
                 reads=[kps, "cs%d" % s2, "st%d_%d" % (s2, si)], writes=[kt[3]])
            if dbg & 4:
                return
            S.op("dve", lambda e: e.tensor_tensor(out=dv[:, :, 0:8], in0=t[0], in1=t[1], op=ALU.subtract),
                 reads=[kt[0], kt[1]], writes=["st%d_%d" % (s2, si)])
            S.op("dve", lambda e: e.tensor_tensor(out=dv[:, :, 8:16], in0=t[2], in1=t[3], op=ALU.add),
                 reads=[kt[2], kt[3]], writes=["st%d_%d" % (s2, si)])
            if si == 2 and not (dbg & 2):
                kd = st[s2][:, 1024:1536].rearrange("p (h t d) -> p h t d", h=4, t=2)
                S.op("pool", lambda e: e.tensor_copy(out=kd[:, :, 1, :], in_=kd[:, :, 0, :]),
                     reads=["st%d_%d" % (s2, si)], writes=["st%d_kd" % s2])

        def post(b):
            s2 = b % 2
            S.op("sp", lambda e: e.dma_start(out=self.qkv_scr[b * P:(b + 1) * P, 0:1792], in_=st[s2]),
                 reads=["st%d_0" % s2, "st%d_1" % s2, "st%d_2" % s2, "st%d_v" % s2, "st%d_kd" % s2],
                 dma="st%d" % s2)

        self.proj_stage(li, self.sw_w_qkv[0], 1536, evac, post)

    def out_proj_block(self, gb_, otok, kotok, oT, koT, wo, xin_t, kxin, xo_t, kxo, chan, pti=0):
        S = self.S
        ptt, kpt = self.pt[pti], "pt%d" % pti
        kotoks = kotok if isinstance(kotok, list) else [kotok]
        for kc in range(8):
            S.op("pe", (lambda e, kc=kc: e.transpose(out=ptt[:, kc * P:(kc + 1) * P],
                                                     in_=otok[:, kc * P:(kc + 1) * P], identity=self.ident)),
                 reads=kotoks + ["ident"], writes=[kpt])
        S.op("act", lambda e: e.copy(out=oT.rearrange("p a b -> p (a b)"), in_=ptt),
             reads=[kpt], writes=[koT])
        O = [self.bank[6], self.bank[7]]
        for kc in range(8):
            for hf in range(2):
                S.op("pe", (lambda e, kc=kc, hf=hf: e.matmul(
                    out=O[hf], lhsT=oT[:, kc, :], rhs=wo[:, kc, hf * 512:(hf + 1) * 512],
                    start=(kc == 0), stop=(kc == 7))),
                    reads=[koT, "wo"], writes=["O%d" % hf])
        for hf in range(2):
            S.op("dve", (lambda e, hf=hf: e.tensor_tensor(
                out=xo_t[:, hf * 512:(hf + 1) * 512], in0=O[hf], in1=xin_t[:, hf * 512:(hf + 1) * 512], op=ALU.add)),
                reads=["O%d" % hf, kxin], writes=[kxo + "_%d" % hf])
        S.op("sp", lambda e: e.dma_start(out=self.y[gb_ * P:(gb_ + 1) * P, :], in_=xo_t),
             reads=[kxo + "_0", kxo + "_1"], dma=chan)

    def swa_mix(self, li):
        nc, S = self.nc, self.S
        wo = self.alloc([8, D], BF16)
        self.load_weight(wo, self.sw_w_o[0], 8, "wo", "w1")
        NR = 4
        qkv = [self.alloc([1792], BF16) for i in range(3)]
        qT = [self.alloc([8, P], BF16) for i in range(2)]
        kTa = [self.alloc([4, P], BF16) for i in range(NR)]
        kTb = [self.alloc([4, P], BF16) for i in range(NR)]
        va = [self.alloc([4, 65], BF16) for i in range(NR)]
        PT = [self.alloc([384], BF16) for i in range(3)]
        otok = [self.alloc([D], BF16) for i in range(2)]
        oT = [self.alloc([8, P], BF16) for i in range(2)]
        xin = [self.alloc([D], F32) for i in range(2)]
        xo = [self.alloc([D], F32) for i in range(2)]
        den = [self.alloc([8], F32) for i in range(2)]
        esk = self.alloc([16], F32)
        self.load_bcast(esk, self.sw_sinks[0:1, :], "esk")
        S.op("act", lambda e: e.activation(out=esk, in_=esk, func=AF.Exp), reads=["esk"], writes=["esk"])
        for i in range(NR):
            S.op("pool", (lambda e, i=i: e.memset(kTa[i], 0.0)), writes=["kTa%d" % i])
            S.op("pool", (lambda e, i=i: e.memset(kTb[i], 0.0)), writes=["kTb%d" % i])
            S.op("pool", (lambda e, i=i: e.memset(va[i], 1.0)), writes=["va%d" % i])
        ST = [self.bank[0], self.bank[1]]
        OT = [self.bank[2], self.bank[3]]
        cnt = {"st": 0, "pt": 0, "ot": 0}

        def load_blk(g):
            s3 = g % 3
            S.op("sp", lambda e: e.dma_start(out=qkv[s3], in_=self.qkv_scr[g * P:(g + 1) * P, 0:1792]),
                 writes=["qkv%d" % s3], dma="qkv%d" % s3)

        def prep_blk(g):
            s3, s2, s4 = g % 3, g % 2, g % NR
            for c in range(8):
                S.op("pe", (lambda e, c=c: e.transpose(out=self.pt[0][:, c * P:(c + 1) * P],
                                                       in_=qkv[s3][:, c * P:(c + 1) * P], identity=self.ident)),
                     reads=["qkv%d" % s3, "ident"], writes=["pt0"])
            S.op("act", lambda e: e.copy(out=qT[s2].rearrange("p a b -> p (a b)"), in_=self.pt[0]),
                 reads=["pt0"], writes=["qT%d" % s2])
            for c in range(4):
                S.op("pe", (lambda e, c=c: e.transpose(out=self.pt[1][:, c * P:(c + 1) * P],
                                                       in_=qkv[s3][:, 1024 + c * P:1024 + (c + 1) * P],
                                                       identity=self.ident)),
                     reads=["qkv%d" % s3, "ident"], writes=["pt1"])
            pv = self.pt[1][:, 0:512].rearrange("p (a b) -> p a b", a=4)
            S.op("act", lambda e: e.copy(out=kTa[s4][0:64], in_=pv[0:64]), reads=["pt1"], writes=["kTa%d" % s4])
            S.op("dve", lambda e: e.tensor_copy(out=kTb[s4][64:128], in_=pv[64:128]), reads=["pt1"],
                 writes=["kTb%d" % s4])
            S.op("pool", lambda e: e.tensor_copy(out=va[s4][:, :, 0:64],
                                                 in_=qkv[s3][:, 1536:1792].rearrange("p (h d) -> p h d", h=4)),
                 reads=["qkv%d" % s3], writes=["va%d" % s4])

        def attn_blk(b0, nb, j):
            g = b0 + j
            s2 = g % 2
            S.op("sp", lambda e: e.dma_start(out=xin[s2], in_=self.y[g * P:(g + 1) * P, :]),
                 writes=["xin%d" % s2], dma="xin%d" % s2)
            kbs = [kb for kb in (j - 1, j, j + 1) if 0 <= kb < nb]
            nk = len(kbs)
            gstate = {}

            def qk_part(hd):
                grp, hi = hd // 4, hd % 4
                c, hh = hd // 2, hd % 2
                kz = kTa if hh == 0 else kTb
                kzn = "kTa" if hh == 0 else "kTb"
                si = cnt["st"] % 2
                cnt["st"] += 1
                pi = cnt["pt"] % 3
                cnt["pt"] += 1
                for r, kb in enumerate(kbs):
                    s4 = (b0 + kb) % NR
                    single = (kb == j)
                    S.op("pe", (lambda e, r=r, s4=s4, single=single: e.matmul(
                        out=ST[si][:, r * P:(r + 1) * P], lhsT=kz[s4][:, grp, :], rhs=qT[s2][:, c, :],
                        start=True, stop=single)),
                        reads=["%s%d" % (kzn, s4), "qT%d" % s2], writes=["ST%d" % si])
                    if not single:
                        mk = self.maskL if kb < j else self.maskR
                        S.op("pe", (lambda e, r=r, mk=mk: e.matmul(
                            out=ST[si][:, r * P:(r + 1) * P], lhsT=self.ident, rhs=mk,
                            start=False, stop=True)),
                            reads=["ident", "maskL", "maskR"], writes=["ST%d" % si])
                S.op("act", lambda e: e.activation(out=PT[pi][:, 0:nk * P], in_=ST[si][:, 0:nk * P],
                                                   func=AF.Exp, scale=0.125),
                     reads=["ST%d" % si], writes=["PT%d" % pi])
                return pi

            def pv_part(hd, pi):
                grp, hi = hd // 4, hd % 4
                if hi == 0:
                    oi = cnt["ot"] % 2
                    cnt["ot"] += 1
                    gstate[grp] = oi
                oi = gstate[grp]
                OTv = OT[oi][:, 0:260].rearrange("p (h d) -> p h d", h=4)
                for r, kb in enumerate(kbs):
                    s4 = (b0 + kb) % NR
                    S.op("pe", (lambda e, r=r, s4=s4: e.matmul(
                        out=OTv[:, hi, :], lhsT=PT[pi][:, r * P:(r + 1) * P], rhs=va[s4][:, grp, :],
                        start=(r == 0), stop=(r == nk - 1))),
                        reads=["PT%d" % pi, "va%d" % s4], writes=["OT%d" % oi])
                if hi == 3:
                    dn = den[oi]
                    S.op("dve", lambda e: e.tensor_tensor(out=dn[:, 0:4], in0=OTv[:, :, 64], in1=esk[:, grp * 4:grp * 4 + 4],
                                                          op=ALU.add),
                         reads=["OT%d" % oi, "esk"], writes=["den%d" % oi])
                    S.op("dve", lambda e: e.reciprocal(out=dn[:, 4:8], in_=dn[:, 0:4]),
                         reads=["den%d" % oi], writes=["rden%d" % oi])
                    S.op("dve", lambda e: e.tensor_tensor(
                        out=otok[s2][:, grp * 256:(grp + 1) * 256].rearrange("p (h d) -> p h d", h=4),
                        in0=OTv[:, :, 0:64], in1=dn[:, 4:8].unsqueeze(2).broadcast_to([P, 4, 64]), op=ALU.mult),
                        reads=["OT%d" % oi, "rden%d" % oi], writes=["otok%d_%d" % (s2, grp)])

            pis = {0: qk_part(0)}
            for hd in range(16):
                if hd + 1 < 16:
                    pis[hd + 1] = qk_part(hd + 1)
                pv_part(hd, pis[hd])
            self.out_proj_block(g, otok[s2], ["otok%d_%d" % (s2, gq) for gq in range(4)], oT[s2], "oT%d" % s2, wo, xin[s2], "xin%d" % s2,
                                xo[s2], "xo%d" % s2, "xo%d" % s2)

        for si_, b0, nb in self.seq_blocks():
            load_blk(b0)
            for i in range(nb):
                if i + 1 < nb:
                    load_blk(b0 + i + 1)
                prep_blk(b0 + i)
                if i >= 1:
                    attn_blk(b0, nb, i - 1)
            attn_blk(b0, nb, nb - 1)
        S.barrier()
        self.free_stage()

    def hg_proj(self, li, ia):
        nc, S = self.nc, self.S
        NB = self.NB
        stg = [self.alloc([8192], BF16) for i in range(2)]
        lbb = [self.alloc([D], F32) for i in range(2)]
        omlb = [self.alloc([D], F32) for i in range(2)]
        q_sbs = [self.alloc([D], F32) for i in range(2)]
        q_sb = q_sbs[0]
        fl = [self.alloc([D], F32) for i in range(2)]
        kk = [self.alloc([D], F32) for i in range(2)]
        E1 = self.alloc([D], F32)
        E2 = self.alloc([D], F32)
        qe_toks = [self.alloc([D], BF16) for i in range(2)]
        sg_tok = self.alloc([D], BF16)
        sgf1 = self.alloc([512], F32)
        sgf = [sgf1, sgf1]
        gnb = self.alloc([D], F32)
        Mm = [self.alloc([P], F32) for i in range(2)]
        Msel = self.alloc([8], F32)
        ar_sb = [self.alloc([64], F32) for i in range(2)]
        PSS = [self.bank[6], self.bank[7]]
        PAR = self.bank[5]

        def setup():
            self.load_bcast(gnb, self.hg_g_norm[ia:ia + 1, :], "gnb")
            S.op("sp", lambda e: e.dma_start(out=Mm[0], in_=self.consts[:, 704:832]), writes=["Mm0"], dma="cst")
            S.op("sp", lambda e: e.dma_start(out=Mm[1], in_=self.consts[:, 832:960]), writes=["Mm1"], dma="cst")
            S.op("sp", lambda e: e.dma_start(out=Msel, in_=self.consts[:, 960:968]), writes=["Msel"], dma="cst")
            tl = [E1, E2, q_sb, fl[0]]
            for d in range(2):
                def one(d=d):
                    for l in range(4):
                        S.op("sp", (lambda e, l=l: e.dma_start(
                            out=tl[l], in_=self.hg_lb[d, l:l + 1, :].broadcast_to([P, D]))),
                            writes=["tl%d" % l], dma="cst")
                    mx = kk[0]
                    S.op("dve", lambda e: e.tensor_tensor(out=mx, in0=tl[0], in1=tl[1], op=ALU.max),
                         reads=["tl0", "tl1"], writes=["mx"])
                    S.op("dve", lambda e: e.tensor_tensor(out=mx, in0=mx, in1=tl[2], op=ALU.max),
                         reads=["mx", "tl2"], writes=["mx"])
                    S.op("dve", lambda e: e.tensor_tensor(out=mx, in0=mx, in1=tl[3], op=ALU.max),
                         reads=["mx", "tl3"], writes=["mx"])
                    for l in range(4):
                        S.op("dve", (lambda e, l=l: e.tensor_tensor(out=tl[l], in0=tl[l], in1=mx, op=ALU.subtract)),
                             reads=["mx", "tl%d" % l], writes=["tl%d" % l])
                        S.op("act", (lambda e, l=l: e.activation(out=tl[l], in_=tl[l], func=AF.Exp)),
                             reads=["tl%d" % l], writes=["tl%d" % l])
                    sm = kk[1]
                    S.op("dve", lambda e: e.tensor_tensor(out=sm, in0=tl[0], in1=tl[1], op=ALU.add),
                         reads=["tl0", "tl1"], writes=["sm"])
                    S.op("dve", lambda e: e.tensor_tensor(out=sm, in0=sm, in1=tl[2], op=ALU.add),
                         reads=["sm", "tl2"], writes=["sm"])
                    S.op("dve", lambda e: e.tensor_tensor(out=sm, in0=sm, in1=tl[3], op=ALU.add),
                         reads=["sm", "tl3"], writes=["sm"])
                    S.op("dve", lambda e: e.reciprocal(out=sm, in_=sm), reads=["sm"], writes=["sm"])
                    S.op("pool", lambda e: e.memset(lbb[d], 0.0), writes=["lbb%d" % d])
                    for l in range(1, li + 1):
                        S.op("dve", (lambda e, l=l: e.tensor_tensor(out=lbb[d], in0=lbb[d], in1=tl[l], op=ALU.add)),
                             reads=["lbb%d" % d, "tl%d" % l], writes=["lbb%d" % d])
                    S.op("dve", lambda e: e.tensor_tensor(out=lbb[d], in0=lbb[d], in1=sm, op=ALU.mult),
                         reads=["lbb%d" % d, "sm"], writes=["lbb%d" % d])
                    S.op("dve", lambda e: e.tensor_scalar(out=omlb[d], in0=lbb[d], scalar1=-1.0, scalar2=1.0,
                                                          op0=ALU.mult, op1=ALU.add),
                         reads=["lbb%d" % d], writes=["omlb%d" % d])
                one()
            for nms, t in ((["E1_0", "E1_1", "E1"], E1), (["E2_0", "E2_1", "E2"], E2), (["q_sb0"], q_sb),
                           (["fl0"], fl[0]), (["kk0"], kk[0]), (["kk1"], kk[1])):
                S.op("pool", (lambda e, t=t: e.tensor_copy(out=t[:, 0:1], in_=t[:, 0:1])),
                     reads=["tl0", "tl1", "tl2", "tl3", "mx", "sm", "omlb0", "omlb1"], writes=nms)

        d_sg, d_scan, d_scan2, d_tr = [], [], [], []

        def flushq(q):
            while q:
                q.pop(0)()

        def flush_all():
            flushq(d_sg)
            flushq(d_scan)
            flushq(d_scan2)
            flushq(d_tr)

        def transposes(src, ksrc, pti, dst, kdst, eng):
            for c in range(8):
                S.op("pe", (lambda e, c=c: e.transpose(out=self.pt[pti][:, c * P:(c + 1) * P],
                                                       in_=src[:, c * P:(c + 1) * P], identity=self.ident)),
                     reads=[ksrc, "ident"], writes=["pt%d" % pti])
            if eng == "act":
                S.op("act", lambda e: e.copy(out=dst, in_=self.pt[pti]), reads=["pt%d" % pti], writes=[kdst])
            else:
                S.op("dve", lambda e: e.tensor_copy(out=dst, in_=self.pt[pti]), reads=["pt%d" % pti], writes=[kdst])

        def gates_dir(b, d):
            s2 = b % 2
            st = stg[s2]
            S.op("pool", lambda e: e.tensor_scalar(out=kk[d], in0=fl[d], scalar1=-1.0, scalar2=1.0,
                                                   op0=ALU.mult, op1=ALU.add),
                 reads=["fl%d" % d], writes=["kk%d" % d])
            S.op("act", lambda e: e.activation(out=fl[d], in_=fl[d], func=AF.Ln),
                 reads=["fl%d" % d, "kk%d" % d], writes=["fl%d" % d])

        def scan_dir(b, d):
            s2 = b % 2
            st = stg[s2]
            for hf in range(2):
                S.op("pe", (lambda e, hf=hf: e.matmul(out=PSS[hf], lhsT=Mm[d], rhs=fl[d][:, hf * 512:(hf + 1) * 512],
                                                     start=True, stop=True)),
                     reads=["Mm%d" % d, "fl%d" % d], writes=["PSS%d" % hf])
            for hd in range(8):
                S.op("pe", (lambda e, hd=hd: e.matmul(out=PAR[:, hd * 4:hd * 4 + 4], lhsT=fl[d][:, hd * P:(hd + 1) * P],
                                                     rhs=Msel[:, d * 4:d * 4 + 4], start=True, stop=True)),
                     reads=["fl%d" % d, "Msel"], writes=["pt1"])
            S.op("act", lambda e: e.activation(out=ar_sb[s2][:, d * 32:(d + 1) * 32], in_=PAR[:, 0:32], func=AF.Exp),
                 reads=["pt1"], writes=["ar%d_%d" % (s2, d)])
            for hf in range(2):
                S.op("act", (lambda e, hf=hf: e.activation(out=E1[:, hf * 512:(hf + 1) * 512], in_=PSS[hf], func=AF.Exp)),
                     reads=["PSS%d" % hf], writes=["E1_%d" % hf])
                S.op("act", (lambda e, hf=hf: e.activation(out=E2[:, hf * 512:(hf + 1) * 512], in_=PSS[hf], func=AF.Exp,
                                                           scale=-1.0)),
                     reads=["PSS%d" % hf], writes=["E2_%d" % hf])
            qe_tok = qe_toks[d]
            S.op("pool", lambda e: e.tensor_tensor(out=qe_tok, in0=q_sbs[s2], in1=E1, op=ALU.mult),
                 reads=["q_sb%d" % s2, "E1_0", "E1_1", "E1"], writes=["qe_tok%d" % d, "E1"])
            ke = st[:, 1024 + d * 1024:2048 + d * 1024]
            S.op("dve", lambda e: e.tensor_tensor(out=ke, in0=kk[d], in1=E2, op=ALU.mult),
                 reads=["kk%d" % d, "E2_0", "E2_1", "E2"], writes=["st%d_ke%d" % (s2, d), "E2"])

            def tr():
                transposes(qe_tok, "qe_tok%d" % d, 0, st[:, 3072 + d * 2048:4096 + d * 2048], "st%d_qeT%d" % (s2, d), "act")
                transposes(ke, "st%d_ke%d" % (s2, d), 1, st[:, 4096 + d * 2048:5120 + d * 2048],
                           "st%d_keT%d" % (s2, d), "dve")
            d_tr.append(tr)

        def store(b):
            s2 = b % 2
            keys = ["st%d_v0" % s2, "st%d_v1" % s2, "st%d_sgT" % s2]
            for d in range(2):
                keys += ["st%d_ke%d" % (s2, d), "st%d_qeT%d" % (s2, d), "st%d_keT%d" % (s2, d)]
            S.op("sp", lambda e: e.dma_start(out=self.hg_scr[b], in_=stg[s2]), reads=keys, dma="st%d" % s2)
            S.op("sp", lambda e: e.dma_start(out=self.ar_scr[b], in_=ar_sb[s2]),
                 reads=["ar%d_0" % s2, "ar%d_1" % s2], dma="ar%d" % s2)

        def evac(b, si, ps, kps):
            s2 = b % 2
            st = stg[s2]
            if si == 1:
                flushq(d_scan)
            if si == 3:
                flushq(d_scan2)
            if si == 7:
                flushq(d_tr)
            if si < 2:
                S.op("dve", lambda e: e.tensor_copy(out=q_sbs[s2][:, si * 512:(si + 1) * 512], in_=ps),
                     reads=[kps], writes=["q_sb%d" % s2])
            elif si < 4:
                S.op("dve", lambda e: e.tensor_copy(out=st[:, (si - 2) * 512:(si - 1) * 512], in_=ps),
                     reads=[kps], writes=["st%d_v%d" % (s2, si - 2)])
            elif si < 6:
                k = si - 4
                S.op("act", lambda e: e.activation(out=sgf[k], in_=ps, func=AF.Silu), reads=[kps], writes=["sgf"])
                S.op("dve", lambda e: e.tensor_tensor(out=sg_tok[:, k * 512:(k + 1) * 512], in0=sgf[k],
                                                      in1=gnb[:, k * 512:(k + 1) * 512], op=ALU.mult),
                     reads=["sgf", "gnb"], writes=["sg_tok"])
                if si == 5:
                    d_sg.append(lambda: transposes(sg_tok, "sg_tok", 1, st[:, 7168:8192], "st%d_sgT" % s2, "dve"))
            else:
                d = (si - 6) // 2
                k = (si - 6) % 2
                cs_ = slice(k * 512, (k + 1) * 512)
                S.op("act", lambda e: e.activation(out=fl[d][:, cs_], in_=ps, func=AF.Sigmoid),
                     reads=[kps], writes=["fl%d" % d])
                S.op("dve", lambda e: e.tensor_tensor(out=fl[d][:, cs_], in0=fl[d][:, cs_], in1=omlb[d][:, cs_], op=ALU.mult),
                     reads=["fl%d" % d, "omlb%d" % d], writes=["fl%d" % d])
                S.op("dve", lambda e: e.tensor_tensor(out=fl[d][:, cs_], in0=fl[d][:, cs_], in1=lbb[d][:, cs_], op=ALU.add),
                     reads=["fl%d" % d, "lbb%d" % d], writes=["fl%d" % d])
                if si == 7:
                    flushq(d_sg)
                if si == 9:
                    gates_dir(b, 0)
                    gates_dir(b, 1)

                    def sc():
                        scan_dir(b, 0)

                    def sc2():
                        scan_dir(b, 1)
                        d_tr.append(lambda: store(b))
                    d_scan.append(sc)
                    d_scan2.append(sc2)

        def post(b):
            pass

        self.proj_stage(li, self.hg_w_in[ia], 5 * D, evac, post, extra_setup=setup, finish=flush_all, nx=2)

    def hg_mix(self, li, ia):
        nc, S = self.nc, self.S
        wo = self.alloc([8, D], BF16)
        self.load_weight(wo, self.hg_w_o[ia], 8, "wo", "w1")
        NRI = 4
        vt = [self.alloc([D], BF16) for i in range(NRI)]
        kz0 = [self.alloc([D], BF16) for i in range(NRI)]
        kz1 = [self.alloc([D], BF16) for i in range(NRI)]
        qk = [self.alloc([2 * D], BF16) for i in range(NRI)]
        NX = 4
        sgT = [self.alloc([D], BF16) for i in range(NX)]
        obw = [self.alloc([D], F32) for i in range(NX)]
        xin = [self.alloc([D], F32) for i in range(NX)]
        xo = [self.alloc([D], F32) for i in range(2)]
        osum = [self.alloc([D], F32) for i in range(3)]
        sq = [self.alloc([D], BF16) for i in range(3)]
        ms = [self.alloc([D], F32) for i in range(3)]
        ogf = [self.alloc([8, P], BF16) for i in range(3)]
        Wst = [self.alloc([8, P], F32) for i in range(2)]
        epsb = self.alloc([4], F32)
        Sb = self.alloc([8, 5, P], BF16)
        Am = [self.alloc([8, P], BF16) for i in range(2)]
        Araw = [self.alloc([8, P], BF16) for i in range(2)]
        maxnb = max(L // P for L in self.seqs)
        ar = self.alloc([maxnb, 64], F32)
        cc = [self.alloc([maxnb, 8, 2], F32) for i in range(2)]
        mk = [self.alloc([P], BF16) for i in range(2)]
        ones = self.alloc([P], BF16)
        mstage = self.alloc([256], F32)
        S.op("sp", lambda e: e.dma_start(out=mstage, in_=self.consts[:, 1024:1280]), writes=["mstage"], dma="cst")
        for d in range(2):
            S.op("dve", (lambda e, d=d: e.tensor_copy(out=mk[d], in_=mstage[:, d * P:(d + 1) * P])),
                 reads=["mstage"], writes=["mk%d" % d])
        S.op("pool", lambda e: e.memset(ones, 1.0), writes=["ones"])
        S.op("pool", lambda e: e.memset(epsb, EPS), writes=["epsb"])
        for i in range(NRI):
            S.op("pool", (lambda e, i=i: e.memset(kz0[i], 0.0)), writes=["kz0_%d" % i])
            S.op("pool", (lambda e, i=i: e.memset(kz1[i], 0.0)), writes=["kz1_%d" % i])
        PA = [self.bank[0], self.bank[1]]
        PU = [self.bank[2], self.bank[3]]
        POB = [self.bank[4], self.bank[5]]
        O = [self.bank[6], self.bank[7]]
        cnt = {"tmp": 0, "ld": 0, "x": 0}

        def run_pass(b0, nb, d):
            order = [0, 1] if d == 0 else [1, 0]
            blocks = list(range(nb)) if d == 0 else list(range(nb - 1, -1, -1))
            c = cc[d]
            slot_of = {}

            def load_main(ii):
                if ii >= nb:
                    return
                g = b0 + blocks[ii]
                sl = cnt["ld"] % NRI
                cnt["ld"] += 1
                slot_of[ii] = sl
                src = self.hg_scr[g]
                S.op("sp", lambda e: e.dma_start(out=vt[sl], in_=src[:, 0:1024]), writes=["vt%d" % sl], dma="hv%d" % sl)
                kc0 = 1024 + d * 1024
                S.op("sp", lambda e: e.dma_start(out=kz0[sl][0:64, :], in_=src[0:64, kc0:kc0 + 1024]),
                     writes=["kz0_%d" % sl], dma="hk%d" % sl)
                S.op("sp", lambda e: e.dma_start(out=kz1[sl][64:128, :], in_=src[64:128, kc0:kc0 + 1024]),
                     writes=["kz1_%d" % sl], dma="hj%d" % sl)
                qc0 = 3072 + d * 2048
                S.op("sp", lambda e: e.dma_start(out=qk[sl], in_=src[:, qc0:qc0 + 2048]), writes=["qk%d" % sl],
                     dma="hq%d" % sl)

            xslot = {}

            def load_extra(ii):
                if ii >= nb or d == 1:
                    return
                g = b0 + blocks[ii]
                sx = cnt["x"] % NX
                cnt["x"] += 1
                xslot[ii] = sx
                S.op("sp", lambda e: e.dma_start(out=sgT[sx], in_=self.hg_scr[g][:, 7168:8192]), writes=["sgT%d" % sx],
                     dma="hx%d" % sx)
                S.op("sp", lambda e: e.dma_start(out=obw[sx], in_=self.obw_scr[g]), reads=["obwd%d" % g],
                     writes=["obw%d" % sx], dma="hx%d" % sx)
                S.op("sp", lambda e: e.dma_start(out=xin[sx], in_=self.y[g * P:(g + 1) * P, :]), writes=["xin%d" % sx],
                     dma="hx%d" % sx)

            S.op("sp", lambda e: e.dma_start(out=ar[:, 0:nb, :], in_=self.ar_scr[b0:b0 + nb].rearrange("b p c -> p b c")),
                 writes=["ar"], dma="cst")
            arv = ar.rearrange("p b (d h j) -> p b d h j", d=2, h=8)
            S.op("pool", lambda e: e.memset(c, 1.0), writes=["cc%d" % d])
            if d == 0:
                S.op("dve", lambda e: e.tensor_tensor(out=c[:, 0:nb, :, 0], in0=arv[:, 0:nb, 0, :, 0],
                                                      in1=arv[:, 0:nb, 0, :, 3], op=ALU.mult),
                     reads=["ar", "cc%d" % d], writes=["cc%d" % d])
                S.op("dve", lambda e: e.tensor_tensor(out=c[:, 0:nb - 1, :, 1], in0=arv[:, 0:nb - 1, 0, :, 2],
                                                      in1=arv[:, 1:nb, 0, :, 1], op=ALU.mult),
                     reads=["ar", "cc%d" % d], writes=["cc%d" % d])
            else:
                S.op("dve", lambda e: e.tensor_tensor(out=c[:, 0:nb, :, 1], in0=arv[:, 0:nb, 1, :, 2],
                                                      in1=arv[:, 0:nb, 1, :, 1], op=ALU.mult),
                     reads=["ar", "cc%d" % d], writes=["cc%d" % d])
                S.op("dve", lambda e: e.tensor_tensor(out=c[:, 1:nb, :, 0], in0=arv[:, 1:nb, 1, :, 0],
                                                      in1=arv[:, 0:nb - 1, 1, :, 3], op=ALU.mult),
                     reads=["ar", "cc%d" % d], writes=["cc%d" % d])
            S.op("pool", lambda e: e.memset(Wst[1], 0.0), writes=["W1_%d" % h_ for h_ in range(8)])
            S.op("pool", lambda e: e.memset(Sb[:, :, 0, :], 0.0), writes=["Sb%d_0" % h_ for h_ in range(8)])

            def P1(ii):
                blk = blocks[ii]
                sl = slot_of[ii]
                par = ii % 2
                last_block = (ii == nb - 1)

                def head_mm(h):
                    hs = slice(h * P, (h + 1) * P)
                    S.op("pe", lambda e: e.matmul(out=PA[h % 2][:, 0:P], lhsT=qk[sl][:, D + h * P:D + (h + 1) * P],
                                                  rhs=qk[sl][:, hs], start=True, stop=True),
                         reads=["qk%d" % sl], writes=["PA%d" % (h % 2)])
                    S.op("act", lambda e: e.copy(out=Araw[par][:, h, :], in_=PA[h % 2][:, 0:P]),
                         reads=["PA%d" % (h % 2)], writes=["Ar%d_%d" % (par, h)])
                    S.op("pool", lambda e: e.tensor_tensor(out=Am[par][:, h, :], in0=Araw[par][:, h, :], in1=mk[d], op=ALU.mult),
                         reads=["Ar%d_%d" % (par, h), "mk%d" % d], writes=["Am%d_%d" % (par, h)])
                    S.op("pe", lambda e: e.matmul(out=PU[h % 2][:, 0:P], lhsT=kz0[sl][:, hs], rhs=vt[sl][:, hs],
                                                  start=True, stop=True),
                         reads=["kz0_%d" % sl, "vt%d" % sl], writes=["PU%d" % (h % 2)])
                    S.op("pe", lambda e: e.matmul(out=PU[h % 2][:, P:2 * P], lhsT=kz1[sl][:, hs], rhs=vt[sl][:, hs],
                                                  start=True, stop=True),
                         reads=["kz1_%d" % sl, "vt%d" % sl], writes=["PU%d" % (h % 2)])

                def head_step(h, j):
                    if last_block and j == 1:
                        return
                    p = order[j]
                    k_out = 2 * ii + j + 1
                    cs_ = c[:, blk, h, p:p + 1]
                    if j == 1:
                        cprev = c[:, blk, h, order[0]:order[0] + 1]
                    elif ii == 0:
                        cprev = 1.0
                    else:
                        pb_ = blocks[ii - 1]
                        cprev = c[:, pb_, h, order[1]:order[1] + 1]
                    wo_, wi_ = (k_out - 1) % 2, k_out % 2
                    S.op("dve", lambda e: e.scalar_tensor_tensor(
                        out=Wst[wo_][:, h, :], in0=Wst[wi_][:, h, :], scalar=cprev,
                        in1=PU[h % 2][:, p * P:(p + 1) * P], op0=ALU.mult, op1=ALU.add),
                        reads=["PU%d" % (h % 2), "W%d_%d" % (wi_, h), "cc%d" % d], writes=["W%d_%d" % (wo_, h)])
                    S.op("act", lambda e: e.activation(out=Sb[:, h, k_out % 5, :], in_=Wst[wo_][:, h, :], func=AF.Copy,
                                                       scale=cs_),
                         reads=["W%d_%d" % (wo_, h), "cc%d" % d], writes=["Sb%d_%d" % (h, k_out % 5)])

                for hp in range(4):
                    head_mm(2 * hp)
                    head_mm(2 * hp + 1)
                    for j in range(2):
                        head_step(2 * hp, j)
                        head_step(2 * hp + 1, j)

            def P2(ii):
                sl = slot_of[ii]
                par = ii % 2

                def head(h):
                    bk = h // 4
                    c0 = (h % 4) * P
                    hs = slice(h * P, (h + 1) * P)
                    S.op("pe", lambda e: e.matmul(out=POB[bk][:, c0:c0 + P], lhsT=vt[sl][:, hs], rhs=Am[par][:, h, :],
                                                  start=True, stop=False),
                         reads=["vt%d" % sl, "Am%d_%d" % (par, h)], writes=["POB%d" % bk])
                    for j in range(2):
                        p = order[j]
                        k_in = 2 * ii + j
                        S.op("pe", (lambda e, p=p, k_in=k_in, j=j: e.matmul(
                            out=POB[bk][:, c0 + p * 64:c0 + (p + 1) * 64], lhsT=Sb[:, h, k_in % 5, :],
                            rhs=qk[sl][:, h * P + p * 64:h * P + (p + 1) * 64], start=False, stop=(j == 1))),
                            reads=["Sb%d_%d" % (h, k_in % 5), "qk%d" % sl], writes=["POB%d" % bk])
                for h in range(8):
                    head(h)

            def evac_bwd(ii):
                g = b0 + blocks[ii]
                so = ii % 2
                for bk in range(2):
                    S.op("act", (lambda e, bk=bk: e.copy(out=obw[so][:, bk * 512:(bk + 1) * 512], in_=POB[bk])),
                         reads=["POB%d" % bk], writes=["obw%d" % so])
                S.op("sp", lambda e: e.dma_start(out=self.obw_scr[g], in_=obw[so]),
                     reads=["obw%d" % so], writes=["obwd%d" % g], dma="hx%d" % so)

            def tail0(ii):
                sx = xslot[ii]
                s3 = ii % 3
                for bk in range(2):
                    S.op("dve", (lambda e, bk=bk: e.tensor_tensor(out=osum[s3][:, bk * 512:(bk + 1) * 512], in0=POB[bk],
                                                                  in1=obw[sx][:, bk * 512:(bk + 1) * 512], op=ALU.add)),
                         reads=["POB%d" % bk, "obw%d" % sx], writes=["osum%d_%d" % (s3, bk)])
                S.op("act", lambda e: e.activation(out=sq[s3], in_=osum[s3], func=AF.Square),
                     reads=["osum%d_0" % s3, "osum%d_1" % s3], writes=["sq%d" % s3])

            def tailA(ii):
                sx = xslot[ii]
                s3 = ii % 3
                for hf in range(2):
                    S.op("pe", (lambda e, hf=hf: e.matmul(out=O[hf], lhsT=ones, rhs=sq[s3][:, hf * 512:(hf + 1) * 512],
                                                         start=True, stop=True)),
                         reads=["ones", "sq%d" % s3], writes=["O%d" % hf])
                    S.op("act", (lambda e, hf=hf: e.activation(out=ms[s3][:, hf * 512:(hf + 1) * 512], in_=O[hf],
                                                               func=AF.Sqrt, scale=1.0 / 128.0, bias=epsb[:, 0:1])),
                         reads=["O%d" % hf, "epsb"], writes=["ms%d_%d" % (s3, hf)])
                mk_ = ["ms%d_0" % s3, "ms%d_1" % s3]
                ok_ = ["osum%d_0" % s3, "osum%d_1" % s3]
                S.op("dve", lambda e: e.reciprocal(out=ms[s3], in_=ms[s3]), reads=mk_, writes=mk_)
                S.op("dve", lambda e: e.tensor_tensor(out=osum[s3], in0=osum[s3], in1=ms[s3], op=ALU.mult),
                     reads=ok_ + mk_, writes=ok_)
                S.op("pool", lambda e: e.tensor_tensor(out=ogf[s3].rearrange("p a b -> p (a b)"), in0=osum[s3], in1=sgT[sx],
                                                       op=ALU.mult),
                     reads=ok_ + ["sgT%d" % sx], writes=["ogf%d" % s3])

            def tailB(ii):
                g = b0 + blocks[ii]
                sx = xslot[ii]
                s3 = ii % 3
                s2 = ii % 2
                for kc in range(8):
                    for hf in range(2):
                        S.op("pe", (lambda e, kc=kc, hf=hf: e.matmul(
                            out=O[hf], lhsT=ogf[s3][:, kc, :], rhs=wo[:, kc, hf * 512:(hf + 1) * 512],
                            start=(kc == 0), stop=(kc == 7))),
                            reads=["ogf%d" % s3, "wo"], writes=["O%d" % hf])
                for hf in range(2):
                    S.op("dve", (lambda e, hf=hf: e.tensor_tensor(
                        out=xo[s2][:, hf * 512:(hf + 1) * 512], in0=O[hf], in1=xin[sx][:, hf * 512:(hf + 1) * 512],
                        op=ALU.add)),
                        reads=["O%d" % hf, "xin%d" % sx], writes=["xo%d_%d" % (s2, hf)])
                S.op("sp", lambda e: e.dma_start(out=self.y[g * P:(g + 1) * P, :], in_=xo[s2]),
                     reads=["xo%d_0" % s2, "xo%d_1" % s2], dma="xo%d" % s2)

            load_main(0)
            load_main(1)
            load_main(2)
            load_extra(0)
            P1(0)
            for ii in range(nb):
                load_main(ii + 3)
                if ii + 1 < nb:
                    P1(ii + 1)
                P2(ii)
                if d == 1:
                    evac_bwd(ii)
                else:
                    tail0(ii)
                    if ii >= 1:
                        tailA(ii - 1)
                    if ii >= 2:
                        tailB(ii - 2)
                    load_extra(ii + 1)
            if d == 0:
                tailA(nb - 1)
                if nb >= 2:
                    tailB(nb - 2)
                tailB(nb - 1)

        for si_, b0, nb in self.seq_blocks():
            run_pass(b0, nb, 1)
            run_pass(b0, nb, 0)
        S.barrier()
        self.free_stage()

    def na_proj(self, li):
        S = self.S
        st = [self.alloc([3072], BF16) for i in range(2)]

        def evac(b, si, ps, kps):
            s2 = b % 2
            dst = st[s2][:, si * 512:(si + 1) * 512]
            if si % 2 == 0:
                S.op("act", lambda e: e.copy(out=dst, in_=ps), reads=[kps], writes=["st%d_%d" % (s2, si)])
            else:
                S.op("dve", lambda e: e.tensor_copy(out=dst, in_=ps), reads=[kps], writes=["st%d_%d" % (s2, si)])

        def post(b):
            s2 = b % 2
            S.op("sp", lambda e: e.dma_start(out=self.qkv_scr[b * P:(b + 1) * P, :], in_=st[s2]),
                 reads=["st%d_%d" % (s2, k) for k in range(6)], dma="st%d" % s2)

        self.proj_stage(li, self.na_w_qkv[0], 3072, evac, post)

    def na_mix(self, li):
        nc, S = self.nc, self.S
        wo = self.alloc([8, D], BF16)
        self.load_weight(wo, self.na_w_o[0], 8, "wo", "w1")
        NR = 6
        qkv = [self.alloc([3072], BF16) for i in range(2)]
        qT = [self.alloc([8, P], BF16) for i in range(4)]
        kTa = [self.alloc([8, P], BF16) for i in range(NR)]
        kTb = [self.alloc([8, P], BF16) for i in range(NR)]
        va = [self.alloc([16, 65], BF16) for i in range(NR)]
        PT = [self.alloc([640], BF16) for i in range(3)]
        otok = [self.alloc([D], BF16) for i in range(2)]
        oT = [self.alloc([8, P], BF16) for i in range(2)]
        xin = [self.alloc([D], F32) for i in range(2)]
        xo = [self.alloc([D], F32) for i in range(2)]
        den = [self.alloc([4], F32) for i in range(3)]
        T2 = self.alloc([16, 15, 64], BF16)
        stg = [self.alloc([15, 64], F32) for i in range(2)]
        cmask = self.alloc([64], F32)
        S.op("sp", lambda e: e.dma_start(out=cmask, in_=self.consts[:, 640:704]), writes=["cmask"], dma="cst")
        for i in range(2):
            S.op("pool", (lambda e, i=i: e.memset(stg[i], 0.0)), writes=["stg%d" % i, "stg%db" % i])
        for hd in range(16):
            i = hd % 2
            S.op("sp", (lambda e, hd=hd, i=i: e.dma_start(out=stg[i][0:64, :, :],
                                                           in_=self.na_bias[hd].rearrange("j k q -> k j q"))),
                 writes=["stg%d" % i], dma="stg%da" % i)
            S.op("sp", (lambda e, hd=hd, i=i: e.dma_start(out=stg[i][64:128, 1:15, :],
                                                           in_=self.na_bias[hd][0:14].rearrange("j k q -> k j q"))),
                 writes=["stg%db" % i], dma="stg%db" % i)
            S.op("dve", (lambda e, hd=hd, i=i: e.scalar_tensor_tensor(
                out=T2[:, hd, :, :], in0=stg[i], scalar=8.0, in1=cmask.unsqueeze(1).broadcast_to([P, 15, 64]),
                op0=ALU.mult, op1=ALU.add)),
                reads=["stg%d" % i, "stg%db" % i, "cmask"], writes=["T2", "stg%d" % i, "stg%db" % i])
        for i in range(NR):
            S.op("pool", (lambda e, i=i: e.memset(kTa[i], 0.0)), writes=["kTa%d" % i])
            S.op("pool", (lambda e, i=i: e.memset(kTb[i], 0.0)), writes=["kTb%d" % i])
            S.op("pool", (lambda e, i=i: e.memset(va[i], 1.0)), writes=["va%d" % i])
        ST = [self.psum[:, 0:640], self.psum[:, 1024:1664]]
        OT1 = self.bank[4][:, 0:130].rearrange("p (h d) -> p h d", h=2)
        OT = [OT1, OT1, OT1]
        cnt = {"st": 0, "pt": 0, "ot": 0}
        rowmasks = {}

        def get_rowmask(pat):
            if pat not in rowmasks:
                t = self.alloc([P], BF16)
                nm = "rm%d" % len(rowmasks)
                for kr in range(2):
                    for qr in range(2):
                        val = 0.0 if pat[kr * 2 + qr] else NEG
                        S.op("pool", (lambda e, kr=kr, qr=qr, val=val: e.memset(
                            t[kr * 64:(kr + 1) * 64, qr * 64:(qr + 1) * 64], val)), writes=[nm])
                rowmasks[pat] = (t, nm)
            return rowmasks[pat]

        def load_blk(g):
            s2 = g % 2
            S.op("sp", lambda e: e.dma_start(out=qkv[s2], in_=self.qkv_scr[g * P:(g + 1) * P, :]),
                 writes=["qkv%d" % s2], dma="qkv%d" % s2)

        def prep_blk(g):
            s2, s4, s6 = g % 2, g % 4, g % NR
            for c in range(8):
                S.op("pe", (lambda e, c=c: e.transpose(out=self.pt[1][:, c * P:(c + 1) * P],
                                                       in_=qkv[s2][:, c * P:(c + 1) * P], identity=self.ident)),
                     reads=["qkv%d" % s2, "ident"], writes=["pt1"])
            S.op("act", lambda e: e.copy(out=qT[s4].rearrange("p a b -> p (a b)"), in_=self.pt[1]),
                 reads=["pt1"], writes=["qT%d" % s4])
            for c in range(8):
                S.op("pe", (lambda e, c=c: e.transpose(out=self.pt[1][:, c * P:(c + 1) * P],
                                                       in_=qkv[s2][:, 1024 + c * P:1024 + (c + 1) * P],
                                                       identity=self.ident)),
                     reads=["qkv%d" % s2, "ident"], writes=["pt1"])
            pv = self.pt[1].rearrange("p (a b) -> p a b", a=8)
            S.op("act", lambda e: e.copy(out=kTa[s6][0:64], in_=pv[0:64]), reads=["pt1"], writes=["kTa%d" % s6])
            S.op("dve", lambda e: e.tensor_copy(out=kTb[s6][64:128], in_=pv[64:128]), reads=["pt1"],
                 writes=["kTb%d" % s6])
            S.op("pool", lambda e: e.tensor_copy(out=va[s6][:, :, 0:64],
                                                 in_=qkv[s2][:, 2048:3072].rearrange("p (h d) -> p h d", h=16)),
                 reads=["qkv%d" % s2], writes=["va%d" % s6])

        def needed(nb, m):
            R = 2 * nb
            rs = [min(max(2 * m + qr - 4, 0), R - 8) for qr in range(2)]
            kbs = []
            for b in range(nb):
                pat = tuple(1 if rs[qr] <= 2 * b + kr < rs[qr] + 8 else 0 for kr in range(2) for qr in range(2))
                if any(pat):
                    kbs.append((b, pat))
            return kbs

        def attn_blk(b0, nb, m):
            g = b0 + m
            s2 = g % 2
            s4q = g % 4
            S.op("sp", lambda e: e.dma_start(out=xin[s2], in_=self.y[g * P:(g + 1) * P, :]),
                 writes=["xin%d" % s2], dma="xin%d" % s2)
            kbs = needed(nb, m)
            nk = len(kbs)
            assert nk <= 5
            def qk_part(hd):
                pr, hh = hd // 2, hd % 2
                kz = kTa if hh == 0 else kTb
                kzn = "kTa" if hh == 0 else "kTb"
                si = cnt["st"] % 2
                cnt["st"] += 1
                pi = cnt["pt"] % 3
                cnt["pt"] += 1
                for r, (kb, pat) in enumerate(kbs):
                    s6 = (b0 + kb) % NR
                    Dd = 2 * (kb - m)
                    bias = T2[:, hd, 7 - Dd:9 - Dd, :].rearrange("p a b -> p (a b)")
                    full = all(pat)
                    S.op("pe", (lambda e, r=r, s6=s6: e.matmul(
                        out=ST[si][:, r * P:(r + 1) * P], lhsT=kz[s6][:, pr, :], rhs=qT[s4q][:, pr, :],
                        start=True, stop=False)),
                        reads=["%s%d" % (kzn, s6), "qT%d" % s4q], writes=["ST%d" % si])
                    S.op("pe", (lambda e, r=r, bias=bias, full=full: e.matmul(
                        out=ST[si][:, r * P:(r + 1) * P], lhsT=self.ident, rhs=bias,
                        start=False, stop=full)),
                        reads=["ident", "T2"], writes=["ST%d" % si])
                    if not full:
                        rm, rmn = get_rowmask(pat)
                        S.op("pe", (lambda e, r=r, rm=rm: e.matmul(
                            out=ST[si][:, r * P:(r + 1) * P], lhsT=self.ident, rhs=rm,
                            start=False, stop=True)),
                            reads=["ident", rmn], writes=["ST%d" % si])
                S.op("act", lambda e: e.activation(out=PT[pi][:, 0:nk * P], in_=ST[si][:, 0:nk * P],
                                                   func=AF.Exp, scale=0.125),
                     reads=["ST%d" % si], writes=["PT%d" % pi])
                return pi

            def pv_part(hd, pi):
                pr, hh = hd // 2, hd % 2
                oi = 0
                for r, (kb, pat) in enumerate(kbs):
                    s6 = (b0 + kb) % NR
                    S.op("pe", (lambda e, r=r, s6=s6: e.matmul(
                        out=OT[oi][:, hh, :], lhsT=PT[pi][:, r * P:(r + 1) * P], rhs=va[s6][:, hd, :],
                        start=(r == 0), stop=(r == nk - 1))),
                        reads=["PT%d" % pi, "va%d" % s6], writes=["OT"])
                if hh == 1:
                    dn = den[pr % 3]
                    S.op("dve", lambda e: e.reciprocal(out=dn[:, 0:2], in_=OT[oi][:, :, 64]),
                         reads=["OT"], writes=["rden%d" % (pr % 3)])
                    S.op("dve", lambda e: e.tensor_tensor(
                        out=otok[s2][:, pr * 128:(pr + 1) * 128].rearrange("p (h d) -> p h d", h=2),
                        in0=OT[oi][:, :, 0:64], in1=dn[:, 0:2].unsqueeze(2).broadcast_to([P, 2, 64]), op=ALU.mult),
                        reads=["OT", "rden%d" % (pr % 3)], writes=["otok%d_%d" % (s2, pr)])

            pis = {0: qk_part(0)}
            for hd in range(16):
                if hd + 1 < 16:
                    pis[hd + 1] = qk_part(hd + 1)
                pv_part(hd, pis[hd])
            self.out_proj_block(g, otok[s2], ["otok%d_%d" % (s2, gq) for gq in range(8)], oT[s2], "oT%d" % s2, wo, xin[s2], "xin%d" % s2,
                                xo[s2], "xo%d" % s2, "xo%d" % s2, pti=1)

        for si_, b0, nb in self.seq_blocks():
            pending = list(range(nb))
            load_blk(b0)
            for i in range(nb):
                if i + 1 < nb:
                    load_blk(b0 + i + 1)
                for m in pending:
                    for kb, _ in needed(nb, m):
                        assert kb >= i or ((b0 + kb) % NR) != ((b0 + i) % NR), "NA ring too small"
                prep_blk(b0 + i)
                while pending and max(kb for kb, _ in needed(nb, pending[0])) <= i:
                    attn_blk(b0, nb, pending.pop(0))
            assert not pending
        S.barrier()
        self.free_stage()


def build(seqs, stages, dbg=None):
    B = Builder(seqs, dbg=dbg)
    src = B.x
    for st in stages:
        if st[0] == "ffn":
            B.S.new_epoch()
            B.ffn_stage(st[1], st[2], src, final_norm=(len(st) > 3 and st[3]))
            src = B.y
        elif st[0] == "swa":
            B.swa_proj(st[1])
            B.swa_mix(st[1])
        elif st[0] == "hg_proj":
            B.hg_proj(st[1], st[2])
        elif st[0] == "hg":
            B.hg_proj(st[1], st[2])
            B.hg_mix(st[1], st[2])
        elif st[0] == "swa_proj":
            B.swa_proj(st[1])
        elif st[0] == "na_proj":
            B.na_proj(st[1])
        elif st[0] == "na":
            B.na_proj(st[1])
            B.na_mix(st[1])
    B.S.emit()
    B.n_sems = len(B.S.chan_sem)
    return B.nc


def na_bias_layout(na_rpb):
    r = np.asarray(na_rpb, dtype=np.float32)[0]
    kc = np.arange(64)[:, None]
    qc = np.arange(64)[None, :]
    dc = np.clip(kc - qc + 15, 0, 30)
    e = r[:, ::-1, :][:, :, dc]
    return np.ascontiguousarray(e)


def const_tables():
    c = np.zeros((P, 2048), np.float32)
    c[:, 0:128] = np.eye(P, dtype=np.float32)
    kk = np.arange(P)[:, None]
    qq = np.arange(P)[None, :]
    c[:, 128:256] = np.where(kk >= qq, 0.0, NEG)
    c[:, 256:384] = np.where(kk <= qq, 0.0, NEG)
    kc = (np.arange(P) % 64)[:, None]
    qc = np.arange(64)[None, :]
    cs = np.clip(qc - 8, 0, 48)
    c[:, 640:704] = np.where((kc >= cs) & (kc < cs + 16), 0.0, NEG)
    sv = np.arange(P)[:, None]
    tv = np.arange(P)[None, :]
    same = (sv // 64) == (tv // 64)
    sl = sv % 64
    c[:, 704:832] = np.where(same, (sv <= tv).astype(np.float32) - (sl <= 31).astype(np.float32), 0.0)
    c[:, 832:960] = np.where(same, (sv >= tv).astype(np.float32) - (sl >= 32).astype(np.float32), 0.0)
    s1 = np.arange(P)
    c[:, 960] = ((s1 >= 32) & (s1 <= 63))
    c[:, 961] = (s1 <= 31)
    c[:, 962] = (s1 >= 96)
    c[:, 963] = ((s1 >= 64) & (s1 <= 95))
    c[:, 964] = (s1 <= 31)
    c[:, 965] = ((s1 >= 32) & (s1 <= 63))
    c[:, 966] = ((s1 >= 64) & (s1 <= 95))
    c[:, 967] = (s1 >= 96)
    c[:, 1024:1152] = (same & (sv <= tv))
    c[:, 1152:1280] = (same & (sv >= tv))
    return c


def rope_table():
    half = 8
    inv = np.exp(-np.arange(half, dtype=np.float32) * np.float32(2.0 / 16) * np.float32(np.log(500000.0))).astype(np.float32)
    ang = (np.arange(4096, dtype=np.float32)[:, None] * inv[None, :]).astype(np.float32)
    return np.concatenate([np.cos(ang), np.sin(ang)], axis=1).astype(np.float32)


def device_inputs(w):
    im = {}
    for k in ("norm_g", "ffn_w1", "ffn_w2", "hg_w_in", "hg_w_o", "hg_g_norm", "hg_lb", "sw_w_qkv", "sw_w_o",
              "sw_sinks", "na_w_qkv", "na_w_o"):
        im[k] = np.ascontiguousarray(np.asarray(w[k], dtype=np.float32))
    im["final_norm_g"] = np.ascontiguousarray(np.asarray(w["final_norm_g"], dtype=np.float32).reshape(1, D))
    im["na_bias"] = na_bias_layout(w["na_rpb"])
    im["consts"] = const_tables()
    im["rope"] = rope_table()
    return im


FULL_STAGES = [("ffn", 0, 0), ("hg", 0, 0), ("ffn", 0, 1),
               ("ffn", 1, 0), ("swa", 1), ("ffn", 1, 1),
               ("ffn", 2, 0), ("na", 2), ("ffn", 2, 1),
               ("ffn", 3, 0), ("hg", 3, 1), ("ffn", 3, 1, True)]
SEQS = [4096, 2048, 2048]
_NC_CACHE = {}


def kernel(x_prompt, x_sample, norm_g, final_norm_g, ffn_w1, ffn_w2, hg_w_in, hg_w_o, hg_g_norm, hg_lb,
           sw_w_qkv, sw_w_o, sw_sinks, na_w_qkv, na_w_o, na_rpb, _stages=None):
    stages = _stages or FULL_STAGES
    key = repr(stages)
    if key not in _NC_CACHE:
        _NC_CACHE[key] = build(SEQS, stages)
    nc = _NC_CACHE[key]
    w = dict(norm_g=norm_g, final_norm_g=final_norm_g, ffn_w1=ffn_w1, ffn_w2=ffn_w2, hg_w_in=hg_w_in,
             hg_w_o=hg_w_o, hg_g_norm=hg_g_norm, hg_lb=hg_lb, sw_w_qkv=sw_w_qkv, sw_w_o=sw_w_o,
             sw_sinks=sw_sinks, na_w_qkv=na_w_qkv, na_w_o=na_w_o, na_rpb=na_rpb)
    base = device_inputs(w)
    xp = np.asarray(x_prompt, dtype=np.float32)
    xs = np.asarray(x_sample, dtype=np.float32)
    in_maps = []
    for c in range(NCORES):
        im = dict(base)
        im["x"] = np.ascontiguousarray(np.concatenate([xp[c], xs[2 * c], xs[2 * c + 1]], axis=0))
        in_maps.append(im)
    res = run_bass_kernel_spmd(nc, in_maps, core_ids=list(range(NCORES)))
    yp = np.empty((8, 4096, D), np.float32)
    ys = np.empty((16, 2048, D), np.float32)
    for c in range(NCORES):
        y = np.asarray(res.results[c]["y"])
        yp[c] = y[0:4096]
        ys[2 * c] = y[4096:6144]
        ys[2 * c + 1] = y[6144:8192]
    return (yp, ys)
```

```python
import numpy as np
import ml_dtypes
import concourse.bass as bass
import concourse.mybir as mybir
from concourse.bass_utils import run_bass_kernel_spmd

F32 = mybir.dt.float32
BF16 = mybir.dt.bfloat16
AF = mybir.ActivationFunctionType
ALU = mybir.AluOpType
AX = mybir.AxisListType

D = 1024
DFF = 2816
EPS = 1e-6
NCORES = 8
P = 128
ARENA_BYTES = 204 * 1024
NEG = -30000.0


class Op:
    __slots__ = ("eng", "fn", "cdeps", "ddeps", "dma", "val", "has_dep", "seq", "idx", "sem")


COMPUTE = ("pe", "act", "dve", "pool")
ENGS = ("sp", "act", "dve", "pool", "pe")


class Sched:
    def __init__(self, nc):
        self.nc = nc
        self.q = {e: [] for e in ENGS}
        self.lastw = {}
        self.rd_c = {}
        self.rd_d = {}
        self.chan_cnt = {}
        self.chan_sem = {}
        self.chan_last = {}
        self.pending_barrier = {e: None for e in ENGS}
        self.pkeys = set(["G0", "G1", "U0", "U1", "O0", "O1", "pt0", "pt1", "PJ0", "PJ1", "PJ2", "PJ3",
                          "ST0", "ST1", "OT0", "OT1", "OT", "PA0", "PA1", "PU0", "PU1", "POB0", "POB1", "PSS0", "PSS1"])
        self.epoch_sems = None
        self.n_ops = 0
        self.new_epoch()

    def new_epoch(self):
        self.epoch_sems = {e: self.nc.alloc_semaphore("ep%d_%s" % (self.n_ops, e)) for e in COMPUTE}

    def _add_dep(self, o, d):
        if d is None or d is o:
            return
        if d.dma is not None:
            o.ddeps.add(d)
        else:
            cur = o.cdeps.get(d.eng)
            if cur is None or cur.idx < d.idx:
                o.cdeps[d.eng] = d

    def op(self, eng, fn, reads=(), writes=(), dma=None):
        o = Op()
        o.eng = eng
        o.fn = fn
        o.cdeps = {}
        o.ddeps = set()
        o.dma = dma
        o.has_dep = False
        o.seq = None
        o.val = None
        o.idx = len(self.q[eng])
        o.sem = None
        self.n_ops += 1
        pb = self.pending_barrier[eng]
        if pb is not None:
            for d in pb:
                self._add_dep(o, d)
            self.pending_barrier[eng] = None
        for r in reads:
            self._add_dep(o, self.lastw.get(r))
            if r in self.pkeys:
                for e2, d in self.rd_c.get(r, {}).items():
                    if e2 != eng:
                        self._add_dep(o, d)
        for w in writes:
            self._add_dep(o, self.lastw.get(w))
            for d in self.rd_c.get(w, {}).values():
                self._add_dep(o, d)
            for d in self.rd_d.get(w, ()):
                self._add_dep(o, d)
        for r in reads:
            if dma is not None:
                self.rd_d.setdefault(r, []).append(o)
            else:
                self.rd_c.setdefault(r, {})[eng] = o
        for w in writes:
            self.lastw[w] = o
            self.rd_c[w] = {}
            self.rd_d[w] = []
        if dma is not None:
            self._add_dep(o, self.chan_last.get(dma))
            c = self.chan_cnt.get(dma, 0) + 16
            self.chan_cnt[dma] = c
            o.val = c
            if dma not in self.chan_sem:
                self.chan_sem[dma] = self.nc.alloc_semaphore("ch_" + str(dma))
            o.sem = self.chan_sem[dma]
            self.chan_last[dma] = o
        else:
            o.sem = self.epoch_sems[eng]
        self.q[eng].append(o)
        return o

    def barrier(self):
        deps = []
        for e in COMPUTE:
            for o in reversed(self.q[e]):
                if o.dma is None:
                    deps.append(o)
                    break
        deps.extend(self.chan_last.values())
        for e in ENGS:
            old = self.pending_barrier[e]
            self.pending_barrier[e] = list(deps) + (list(old) if old else [])

    def emit(self, final_waits=True):
        nc = self.nc
        for e in ENGS:
            for o in self.q[e]:
                for d in o.cdeps.values():
                    d.has_dep = True
        cnt = {}
        for e in ENGS:
            for o in self.q[e]:
                if o.dma is None and o.has_dep:
                    k = o.sem.num
                    cnt[k] = cnt.get(k, 0) + 1
                    o.seq = cnt[k]
        sched = self

        def run_engine(ename, eng):
            waited = {}
            for o in sched.q[ename]:
                for d in o.cdeps.values():
                    if d.eng == ename and ename == "pe":
                        continue
                    if waited.get(d.sem.num, 0) < d.seq:
                        eng.wait_ge(d.sem, d.seq)
                        waited[d.sem.num] = d.seq
                for d in o.ddeps:
                    if waited.get(d.sem.num, 0) < d.val:
                        eng.wait_ge(d.sem, d.val)
                        waited[d.sem.num] = d.val
                ins = o.fn(eng)
                if o.dma is not None:
                    ins.then_inc(o.sem, 16)
                elif o.has_dep:
                    ins.then_inc(o.sem, 1)
            if ename == "sp" and final_waits:
                for ch, o in sched.chan_last.items():
                    if waited.get(o.sem.num, 0) < o.val:
                        eng.wait_ge(o.sem, o.val)

        with nc.Block() as block:
            @block.sync
            def _(eng):
                run_engine("sp", eng)

            @block.scalar
            def _(eng):
                run_engine("act", eng)

            @block.vector
            def _(eng):
                run_engine("dve", eng)

            @block.gpsimd
            def _(eng):
                run_engine("pool", eng)

            @block.tensor
            def _(eng):
                run_engine("pe", eng)


class Ring:
    def __init__(self, name, n):
        self.name = name
        self.n = n

    def key(self, i):
        return "%s#%d" % (self.name, i % self.n)

    def slot(self, i):
        return i % self.n


class Builder:
    def __init__(self, seqs, n_layers=4, dbg=None):
        self.seqs = list(seqs)
        self.T = sum(seqs)
        self.NB = self.T // P
        self.n_layers = n_layers
        self.dbg = dbg or {}
        nc = bass.Bass("TRN2", target_bir_lowering=False)
        self.nc = nc
        self.S = Sched(nc)
        T = self.T

        def din(name, shape, dt=F32):
            return nc.dram_tensor(name, list(shape), dt, kind="ExternalInput").ap()

        self.x = din("x", [T, D])
        self.norm_g = din("norm_g", [4, 3, D])
        self.final_norm_g = din("final_norm_g", [1, D])
        self.ffn_w1 = din("ffn_w1", [4, 2, D, 2 * DFF])
        self.ffn_w2 = din("ffn_w2", [4, 2, DFF, D])
        self.hg_w_in = din("hg_w_in", [2, D, 5 * D])
        self.hg_w_o = din("hg_w_o", [2, D, D])
        self.hg_g_norm = din("hg_g_norm", [2, D])
        self.hg_lb = din("hg_lb", [2, 4, D])
        self.sw_w_qkv = din("sw_w_qkv", [1, D, 1536])
        self.sw_w_o = din("sw_w_o", [1, D, D])
        self.sw_sinks = din("sw_sinks", [1, 16])
        self.na_w_qkv = din("na_w_qkv", [1, D, 3072])
        self.na_w_o = din("na_w_o", [1, D, D])
        self.na_bias = din("na_bias", [16, 15, 64, 64])
        self.consts = din("consts", [P, 2048])
        self.y = nc.dram_tensor("y", [T, D], F32, kind="ExternalOutput").ap()

        self.psum = nc.alloc_psum_tensor("psum", [P, 4096], F32).ap()
        self.bank = [self.psum[:, i * 512:(i + 1) * 512] for i in range(8)]
        self.ps = [self.bank[0], self.bank[1], self.bank[2], self.bank[3], self.bank[6], self.bank[7]]
        self.pt = [self.bank[4].bitcast(BF16), self.bank[5].bitcast(BF16)]
        self.rope = din("rope", [4096, 16])
        self.qkv_scr = nc.dram_tensor("qkv_scr", [T, 3072], BF16, kind="Internal").ap()
        self.hg_scr = nc.dram_tensor("hg_scr", [T // P, P, 8192], BF16, kind="Internal").ap()
        self.ar_scr = nc.dram_tensor("ar_scr", [T // P, P, 64], F32, kind="Internal").ap()
        self.obw_scr = nc.dram_tensor("obw_scr", [T // P, P, D], F32, kind="Internal").ap()

        self.arena = nc.alloc_sbuf_tensor("arena", [P, ARENA_BYTES // 2], BF16).ap()
        self.a_off = 0
        self.a_mark = 0
        self.ident = self.alloc([P], BF16)
        self.identf = self.alloc([P], F32)
        self.neghalf = self.alloc([4], F32)
        self.maskL = self.alloc([P], BF16)
        self.maskR = self.alloc([P], BF16)
        self.cstage = self.alloc([512], F32)
        self.a_mark = self.a_off
        self.S.op("sp", lambda e: e.dma_start(out=self.cstage, in_=self.consts[:, 128:640]),
                  writes=["cstage"], dma="cst")
        self.S.op("dve", lambda e: e.tensor_copy(out=self.maskL, in_=self.cstage[:, 0:128]),
                  reads=["cstage"], writes=["maskL"])
        self.S.op("dve", lambda e: e.tensor_copy(out=self.maskR, in_=self.cstage[:, 128:256]),
                  reads=["cstage"], writes=["maskR"])
        self.S.op("pool", lambda e: e.memset(self.neghalf, -0.5), writes=["neghalf"])
        S = self.S
        S.op("sp", lambda e: e.dma_start(out=self.identf, in_=self.consts[:, 0:128]),
             writes=["identf"], dma="cst")
        S.op("dve", lambda e: e.tensor_copy(out=self.ident, in_=self.identf),
             reads=["identf"], writes=["ident"])

    def alloc(self, fshape, dt):
        n = 1
        for v in fshape:
            n *= v
        nbytes = n * (4 if dt == F32 else 2)
        nbytes = (nbytes + 63) // 64 * 64
        assert self.a_off + nbytes <= ARENA_BYTES, ("SBUF arena overflow", self.a_off, nbytes)
        v = self.arena[:, self.a_off // 2:(self.a_off + nbytes) // 2]
        self.a_off += nbytes
        if dt == F32:
            v = v.bitcast(F32)
        v = v[:, 0:n]
        if len(fshape) == 2:
            v = v.rearrange("p (a b) -> p a b", a=fshape[0])
        elif len(fshape) == 3:
            v = v.rearrange("p (a b c) -> p a b c", a=fshape[0], b=fshape[1])
        return v

    def free_stage(self):
        self.a_off = self.a_mark

    def load_weight(self, sb, dram2d, KC, key, chan):
        S = self.S
        for kc in range(KC):
            S.op("pool", (lambda e, kc=kc: e.dma_start(out=sb[:, kc, :], in_=dram2d[kc * P:(kc + 1) * P, :],
                                                       max_dma_last_dim=8192)),
                 writes=[key], dma=chan)

    def load_bcast(self, sb, dram_row, key, chan="cst"):
        n = dram_row.shape[-1]
        self.S.op("sp", lambda e: e.dma_start(out=sb, in_=dram_row.broadcast_to([P, n])),
                  writes=[key], dma=chan)

    def rsqrt_eps(self, ss, kss):
        S = self.S
        S.op("pool", lambda e: e.tensor_scalar(out=ss[:, 2:3], in0=ss[:, 0:1], scalar1=EPS, scalar2=1.0,
                                               op0=ALU.add, op1=ALU.mult),
             reads=[kss], writes=[kss + "e"])
        S.op("pool", lambda e: e.tensor_tensor(out=ss[:, 1:2], in0=ss[:, 2:3], in1=self.neghalf[:, 0:1], op=ALU.pow),
             reads=[kss + "e", "neghalf"], writes=[kss + "r"])

    def norm_T(self, xt, kx, gb, kg, h, kh, junk, kj, ss, kss, ptile, kpt, hT, khT, evac_eng="act", tr=True):
        S = self.S
        S.op("act", lambda e: e.activation(out=junk, in_=xt, func=AF.Square, scale=float(D ** -0.5),
                                           accum_out=ss[:, 0:1]),
             reads=[kx], writes=[kj, kss])
        self.rsqrt_eps(ss, kss)
        S.op("dve", lambda e: e.scalar_tensor_tensor(out=h, in0=xt, scalar=ss[:, 1:2], in1=gb,
                                                     op0=ALU.mult, op1=ALU.mult),
             reads=[kx, kss + "r", kg], writes=[kh])
        if not tr:
            return
        self.norm_tr(h, kh, ptile, kpt, hT, khT, evac_eng)

    def norm_tr(self, h, kh, ptile, kpt, hT, khT, evac_eng="act"):
        S = self.S
        for kc in range(8):
            S.op("pe", (lambda e, kc=kc: e.transpose(out=ptile[:, kc * P:(kc + 1) * P],
                                                     in_=h[:, kc * P:(kc + 1) * P], identity=self.ident)),
                 reads=[kh, "ident"], writes=[kpt])
        hT2 = hT.rearrange("p a b -> p (a b)")
        if evac_eng == "act":
            S.op("act", lambda e: e.copy(out=hT2, in_=ptile), reads=[kpt], writes=[khT])
        else:
            S.op("dve", lambda e: e.tensor_copy(out=hT2, in_=ptile), reads=[kpt], writes=[khT])

    def ffn_stage(self, li, j, src, final_norm=False):
        nc, S = self.nc, self.S
        NB = self.NB
        tag = "f%d%d_" % (li, j)
        w1 = self.alloc([8, 2 * DFF], BF16)
        w2 = self.alloc([22, D], BF16)
        gb = self.alloc([D], F32)
        xin = [self.alloc([D], F32) for i in range(3)]
        xo = [self.alloc([D], F32) for i in range(2)]
        h = [self.alloc([D], BF16) for i in range(2)]
        hT = [self.alloc([8, P], BF16) for i in range(2)]
        a = [self.alloc([DFF], BF16) for i in range(2)]
        aT = [self.alloc([22, P], BF16) for i in range(2)]
        sg = [self.alloc([512], F32) for i in range(2)]
        junk = self.alloc([D], BF16)
        ss = [self.alloc([4], F32) for i in range(2)]
        if final_norm:
            gf = self.alloc([D], F32)
            self.load_bcast(gf, self.final_norm_g, "gf")
            ss2 = [self.alloc([4], F32) for i in range(2)]

        self.load_weight(w1, self.ffn_w1[li, j], 8, "w1", "w1")
        self.load_weight(w2, self.ffn_w2[li, j], 22, "w2", "w2")
        self.load_bcast(gb, self.norm_g[li, 2 * j:2 * j + 1, :], "gb")

        slices = [(c, min(512, DFF - c)) for c in range(0, DFF, 512)]
        G = [self.ps[0], self.ps[1]]
        U = [self.ps[2], self.ps[3]]
        O = [self.ps[4], self.ps[5]]

        def load_x(b):
            if b < NB:
                s = b % 3
                S.op("sp", lambda e: e.dma_start(out=xin[s], in_=src[b * P:(b + 1) * P, :]),
                     writes=["xin%d" % s], dma="xin%d" % s)

        def do_norm(b, part=2):
            if b < NB:
                s3, s2 = b % 3, b % 2
                if part in (0, 2):
                    self.norm_T(xin[s3], "xin%d" % s3, gb, "gb", h[s2], "h%d" % s2, junk, "junk",
                                ss[s2], "ss%d" % s2, self.pt[0], "pt0", hT[s2], "hT%d" % s2, tr=(part == 2))
                if part == 1:
                    self.norm_tr(h[s2], "h%d" % s2, self.pt[0], "pt0", hT[s2], "hT%d" % s2)

        load_x(0)
        load_x(1)
        do_norm(0)
        def do_block(b):
            s3, s2 = b % 3, b % 2
            load_x(b + 2)
            akeys = ["a%d_%d" % (s2, si) for si in range(len(slices))]

            def tr_round(r):
                k0 = r * 8
                k1 = min(22, k0 + 8)
                need = sorted(set(akeys[(kc * P) // 512] for kc in range(k0, k1)))
                for kc in range(k0, k1):
                    S.op("pe", (lambda e, kc=kc, k0=k0: e.transpose(
                        out=self.pt[1][:, (kc - k0) * P:(kc - k0 + 1) * P],
                        in_=a[s2][:, kc * P:(kc + 1) * P], identity=self.ident)),
                        reads=need + ["ident"], writes=["pt1"])
                n = k1 - k0
                dst = aT[s2][:, k0:k1, :].rearrange("p a b -> p (a b)")
                eng = "act" if r != 1 else "dve"
                if eng == "act":
                    S.op("act", lambda e: e.copy(out=dst, in_=self.pt[1][:, 0:n * P]),
                         reads=["pt1"], writes=["aT%d_%d" % (s2, r)])
                else:
                    S.op("dve", lambda e: e.tensor_copy(out=dst, in_=self.pt[1][:, 0:n * P]),
                         reads=["pt1"], writes=["aT%d_%d" % (s2, r)])

            for si, (c0, w) in enumerate(slices):
                g = si % 2
                for kc in range(8):
                    S.op("pe", (lambda e, kc=kc, c0=c0, w=w, g=g: e.matmul(
                        out=G[g][:, 0:w], lhsT=hT[s2][:, kc, :], rhs=w1[:, kc, c0:c0 + w],
                        start=(kc == 0), stop=(kc == 7))),
                        reads=["hT%d" % s2, "w1"], writes=["G%d" % g])
                for kc in range(8):
                    S.op("pe", (lambda e, kc=kc, c0=c0, w=w, g=g: e.matmul(
                        out=U[g][:, 0:w], lhsT=hT[s2][:, kc, :], rhs=w1[:, kc, DFF + c0:DFF + c0 + w],
                        start=(kc == 0), stop=(kc == 7))),
                        reads=["hT%d" % s2, "w1"], writes=["U%d" % g])
                S.op("act", (lambda e, w=w, g=g: e.activation(out=sg[g][:, 0:w], in_=G[g][:, 0:w], func=AF.Silu)),
                     reads=["G%d" % g], writes=["sg%d" % g])
                S.op("dve", (lambda e, c0=c0, w=w, g=g: e.tensor_tensor(
                    out=a[s2][:, c0:c0 + w], in0=sg[g][:, 0:w], in1=U[g][:, 0:w], op=ALU.mult)),
                    reads=["sg%d" % g, "U%d" % g], writes=["a%d_%d" % (s2, si)])
                if si == 0:
                    do_norm(b + 1, 0)
                if si == 3:
                    do_norm(b + 1, 1)
                    tr_round(0)

            def phase_b(r):
                k0 = r * 8
                k1 = min(22, k0 + 8)
                for kc in range(k0, k1):
                    for hf in range(2):
                        S.op("pe", (lambda e, kc=kc, hf=hf: e.matmul(
                            out=O[hf], lhsT=aT[s2][:, kc, :], rhs=w2[:, kc, hf * 512:(hf + 1) * 512],
                            start=(kc == 0), stop=(kc == 21))),
                            reads=["aT%d_%d" % (s2, r), "w2"], writes=["O%d" % hf])

            tr_round(1)
            phase_b(0)
            tr_round(2)
            phase_b(1)
            phase_b(2)
            for hf in range(2):
                S.op("dve", (lambda e, hf=hf: e.scalar_tensor_tensor(
                    out=xo[s2][:, hf * 512:(hf + 1) * 512], in0=O[hf], scalar=0.5,
                    in1=xin[s3][:, hf * 512:(hf + 1) * 512], op0=ALU.mult, op1=ALU.add)),
                    reads=["O%d" % hf, "xin%d" % s3], writes=["xo%d_%d" % (s2, hf)])
            okeys = ["xo%d_0" % s2, "xo%d_1" % s2]
            if final_norm:
                S.op("act", lambda e: e.activation(out=junk, in_=xo[s2], func=AF.Square, scale=float(D ** -0.5),
                                                   accum_out=ss2[s2][:, 0:1]),
                     reads=okeys, writes=["junk", "ssf%d" % s2])
                self.rsqrt_eps(ss2[s2], "ssf%d" % s2)
                S.op("dve", lambda e: e.scalar_tensor_tensor(out=xo[s2], in0=xo[s2], scalar=ss2[s2][:, 1:2], in1=gf,
                                                             op0=ALU.mult, op1=ALU.mult),
                     reads=okeys + ["ssf%dr" % s2, "gf"], writes=okeys)
            S.op("sp", lambda e: e.dma_start(out=self.y[b * P:(b + 1) * P, :], in_=xo[s2]),
                 reads=okeys, dma="xo%d" % s2)

        for b in range(NB):
            do_block(b)
        S.barrier()
        self.free_stage()


    def proj_stage(self, li, w_dram, N, evac, post, extra_setup=None, finish=None, nx=3):
        nc, S = self.nc, self.S
        NB = self.NB
        w = self.alloc([8, N], BF16)
        gb = self.alloc([D], F32)
        xin = [self.alloc([D], F32) for i in range(nx)]
        h = [self.alloc([D], BF16) for i in range(2)]
        hT = [self.alloc([8, P], BF16) for i in range(2)]
        junk = self.alloc([D], BF16)
        ss = [self.alloc([4], F32) for i in range(2)]
        self.load_weight(w, w_dram, 8, "wp", "w1")
        self.load_bcast(gb, self.norm_g[li, 1:2, :], "gb")
        if extra_setup is not None:
            extra_setup()
        slices = [(c, min(512, N - c)) for c in range(0, N, 512)]
        src = self.y

        def load_x(b):
            if b < NB:
                s_ = b % nx
                S.op("sp", lambda e: e.dma_start(out=xin[s_], in_=src[b * P:(b + 1) * P, :]),
                     writes=["xin%d" % s_], dma="xin%d" % s_)

        def do_norm(b, part=2):
            if b < NB:
                s3, s2 = b % nx, b % 2
                if part in (0, 2):
                    self.norm_T(xin[s3], "xin%d" % s3, gb, "gb", h[s2], "h%d" % s2, junk, "junk",
                                ss[s2], "ss%d" % s2, self.pt[0], "pt0", hT[s2], "hT%d" % s2, tr=(part == 2))
                if part == 1:
                    self.norm_tr(h[s2], "h%d" % s2, self.pt[0], "pt0", hT[s2], "hT%d" % s2)

        def do_block(b):
            s2 = b % 2
            load_x(b + nx - 1)
            for si, (c0, wd) in enumerate(slices):
                pb = si % 4
                for kc in range(8):
                    S.op("pe", (lambda e, kc=kc, pb=pb, c0=c0, wd=wd: e.matmul(
                        out=self.ps[pb][:, 0:wd], lhsT=hT[s2][:, kc, :], rhs=w[:, kc, c0:c0 + wd],
                        start=(kc == 0), stop=(kc == 7))),
                        reads=["hT%d" % s2, "wp"], writes=["PJ%d" % pb])
                evac(b, si, self.ps[pb][:, 0:wd], "PJ%d" % pb)
                if si == 0:
                    do_norm(b + 1, 0)
                if si == len(slices) - 1:
                    do_norm(b + 1, 1)
            post(b)

        for b_ in range(nx - 1):
            load_x(b_)
        do_norm(0)
        for b in range(NB):
            do_block(b)
        if finish is not None:
            finish()
        S.barrier()
        self.free_stage()

    def seq_blocks(self):
        b0 = 0
        for si, L in enumerate(self.seqs):
            yield si, b0, L // P
            b0 += L // P

    def swa_proj(self, li):
        S = self.S
        st = [self.alloc([1792], BF16) for i in range(2)]
        cs = [self.alloc([16], F32) for i in range(2)]
        tmp = [self.alloc([8, 8], F32) for i in range(4)]
        pos = []
        for L in self.seqs:
            pos.extend(range(0, L, P))

        def evac(b, si, ps, kps):
            s2 = b % 2
            if si == 0:
                p0 = pos[b]
                S.op("sp", lambda e: e.dma_start(out=cs[s2], in_=self.rope[p0:p0 + P, :]),
                     writes=["cs%d" % s2], dma="cs%d" % s2)
            nh = 8 if si < 2 else 4
            if si < 2:
                dst = st[s2][:, si * 512:(si + 1) * 512]
                S.op("act", lambda e: e.copy(out=dst, in_=ps), reads=[kps], writes=["st%d_%d" % (s2, si)])
                dv = dst.rearrange("p (h d) -> p h d", h=8)
            else:
                kd = st[s2][:, 1024:1536].rearrange("p (h t d) -> p h t d", h=4, t=2)
                S.op("act", lambda e: e.copy(out=kd[:, :, 0, :], in_=ps[:, 0:256].rearrange("p (h d) -> p h d", h=4)),
                     reads=[kps], writes=["st%d_%d" % (s2, si)])
                S.op("act", lambda e: e.copy(out=st[s2][:, 1536:1792], in_=ps[:, 256:512]),
                     reads=[kps], writes=["st%d_v" % s2])
                dv = kd[:, :, 0, :]
            import os
            dbg = int(os.environ.get("SWA_DBG", "0"))
            if dbg & 1:
                return
            pv = ps[:, 0:nh * 64].rearrange("p (h d) -> p h d", h=nh)
            x1, x2 = pv[:, :, 0:8], pv[:, :, 8:16]
            cosb = cs[s2][:, 0:8].unsqueeze(1).broadcast_to([P, nh, 8])
            sinb = cs[s2][:, 8:16].unsqueeze(1).broadcast_to([P, nh, 8])
            t = [tt[:, 0:nh, :] for tt in tmp]
            kt = ["rt0", "rt1", "rt2", "rt3"]
            S.op("dve", lambda e: e.tensor_tensor(out=t[0], in0=x1, in1=cosb, op=ALU.mult),
                 reads=[kps, "cs%d" % s2, "st%d_%d" % (s2, si)], writes=[kt[0]])
            S.op("dve", lambda e: e.tensor_tensor(out=t[1], in0=x2, in1=sinb, op=ALU.mult),
                 reads=[kps, "cs%d" % s2, "st%d_%d" % (s2, si)], writes=[kt[1]])
            S.op("dve", lambda e: e.tensor_tensor(out=t[2], in0=x2, in1=cosb, op=ALU.mult),
                 reads=[kps, "cs%d" % s2, "st%d_%d" % (s2, si)], writes=[kt[2]])
            S.op("dve", lambda e: e.tensor_tensor(out=t[3], in0=x1, in1=sinb, op=ALU.mult),
                 reads=[kps, "cs%d" % s2, "st%d_%d" % (s2, si)], writes=[kt[3]])
            if dbg & 4:
                return
            S.op("dve", lambda e: e.tensor_tensor(out=dv[:, :, 0:8], in0=t[0], in1=t[1], op=ALU.subtract),
                 reads=[kt[0], kt[1]], writes=["st%d_%d" % (s2, si)])
            S.op("dve", lambda e: e.tensor_tensor(out=dv[:, :, 8:16], in0=t[2], in1=t[3], op=ALU.add),
                 reads=[kt[2], kt[3]], writes=["st%d_%d" % (s2, si)])
            if si == 2 and not (dbg & 2):
                kd = st[s2][:, 1024:1536].rearrange("p (h t d) -> p h t d", h=4, t=2)
                S.op("pool", lambda e: e.tensor_copy(out=kd[:, :, 1, :], in_=kd[:, :, 0, :]),
                     reads=["st%d_%d" % (s2, si)], writes=["st%d_kd" % s2])

        def post(b):
            s2 = b % 2
            S.op("sp", lambda e: e.dma_start(out=self.qkv_scr[b * P:(b + 1) * P, 0:1792], in_=st[s2]),
                 reads=["st%d_0" % s2, "st%d_1" % s2, "st%d_2" % s2, "st%d_v" % s2, "st%d_kd" % s2],
                 dma="st%d" % s2)

        self.proj_stage(li, self.sw_w_qkv[0], 1536, evac, post)

    def out_proj_block(self, gb_, otok, kotok, oT, koT, wo, xin_t, kxin, xo_t, kxo, chan, pti=0):
        S = self.S
        ptt, kpt = self.pt[pti], "pt%d" % pti
        kotoks = kotok if isinstance(kotok, list) else [kotok]
        for kc in range(8):
            S.op("pe", (lambda e, kc=kc: e.transpose(out=ptt[:, kc * P:(kc + 1) * P],
                                                     in_=otok[:, kc * P:(kc + 1) * P], identity=self.ident)),
                 reads=kotoks + ["ident"], writes=[kpt])
        S.op("act", lambda e: e.copy(out=oT.rearrange("p a b -> p (a b)"), in_=ptt),
             reads=[kpt], writes=[koT])
        O = [self.bank[6], self.bank[7]]
        for kc in range(8):
            for hf in range(2):
                S.op("pe", (lambda e, kc=kc, hf=hf: e.matmul(
                    out=O[hf], lhsT=oT[:, kc, :], rhs=wo[:, kc, hf * 512:(hf + 1) * 512],
                    start=(kc == 0), stop=(kc == 7))),
                    reads=[koT, "wo"], writes=["O%d" % hf])
        for hf in range(2):
            S.op("dve", (lambda e, hf=hf: e.tensor_tensor(
                out=xo_t[:, hf * 512:(hf + 1) * 512], in0=O[hf], in1=xin_t[:, hf * 512:(hf + 1) * 512], op=ALU.add)),
                reads=["O%d" % hf, kxin], writes=[kxo + "_%d" % hf])
        S.op("sp", lambda e: e.dma_start(out=self.y[gb_ * P:(gb_ + 1) * P, :], in_=xo_t),
             reads=[kxo + "_0", kxo + "_1"], dma=chan)

    def swa_mix(self, li):
        nc, S = self.nc, self.S
        wo = self.alloc([8, D], BF16)
        self.load_weight(wo, self.sw_w_o[0], 8, "wo", "w1")
        NR = 4
        qkv = [self.alloc([1792], BF16) for i in range(3)]
        qT = [self.alloc([8, P], BF16) for i in range(2)]
        kTa = [self.alloc([4, P], BF16) for i in range(NR)]
        kTb = [self.alloc([4, P], BF16) for i in range(NR)]
        va = [self.alloc([4, 65], BF16) for i in range(NR)]
        PT = [self.alloc([384], BF16) for i in range(3)]
        otok = [self.alloc([D], BF16) for i in range(2)]
        oT = [self.alloc([8, P], BF16) for i in range(2)]
        xin = [self.alloc([D], F32) for i in range(2)]
        xo = [self.alloc([D], F32) for i in range(2)]
        den = [self.alloc([8], F32) for i in range(2)]
        esk = self.alloc([16], F32)
        self.load_bcast(esk, self.sw_sinks[0:1, :], "esk")
        S.op("act", lambda e: e.activation(out=esk, in_=esk, func=AF.Exp), reads=["esk"], writes=["esk"])
        for i in range(NR):
            S.op("pool", (lambda e, i=i: e.memset(kTa[i], 0.0)), writes=["kTa%d" % i])
            S.op("pool", (lambda e, i=i: e.memset(kTb[i], 0.0)), writes=["kTb%d" % i])
            S.op("pool", (lambda e, i=i: e.memset(va[i], 1.0)), writes=["va%d" % i])
        ST = [self.bank[0], self.bank[1]]
        OT = [self.bank[2], self.bank[3]]
        cnt = {"st": 0, "pt": 0, "ot": 0}

        def load_blk(g):
            s3 = g % 3
            S.op("sp", lambda e: e.dma_start(out=qkv[s3], in_=self.qkv_scr[g * P:(g + 1) * P, 0:1792]),
                 writes=["qkv%d" % s3], dma="qkv%d" % s3)

        def prep_blk(g):
            s3, s2, s4 = g % 3, g % 2, g % NR
            for c in range(8):
                S.op("pe", (lambda e, c=c: e.transpose(out=self.pt[0][:, c * P:(c + 1) * P],
                                                       in_=qkv[s3][:, c * P:(c + 1) * P], identity=self.ident)),
                     reads=["qkv%d" % s3, "ident"], writes=["pt0"])
            S.op("act", lambda e: e.copy(out=qT[s2].rearrange("p a b -> p (a b)"), in_=self.pt[0]),
                 reads=["pt0"], writes=["qT%d" % s2])
            for c in range(4):
                S.op("pe", (lambda e, c=c: e.transpose(out=self.pt[1][:, c * P:(c + 1) * P],
                                                       in_=qkv[s3][:, 1024 + c * P:1024 + (c + 1) * P],
                                                       identity=self.ident)),
                     reads=["qkv%d" % s3, "ident"], writes=["pt1"])
            pv = self.pt[1][:, 0:512].rearrange("p (a b) -> p a b", a=4)
            S.op("act", lambda e: e.copy(out=kTa[s4][0:64], in_=pv[0:64]), reads=["pt1"], writes=["kTa%d" % s4])
            S.op("dve", lambda e: e.tensor_copy(out=kTb[s4][64:128], in_=pv[64:128]), reads=["pt1"],
                 writes=["kTb%d" % s4])
            S.op("pool", lambda e: e.tensor_copy(out=va[s4][:, :, 0:64],
                                                 in_=qkv[s3][:, 1536:1792].rearrange("p (h d) -> p h d", h=4)),
                 reads=["qkv%d" % s3], writes=["va%d" % s4])

        def attn_blk(b0, nb, j):
            g = b0 + j
            s2 = g % 2
            S.op("sp", lambda e: e.dma_start(out=xin[s2], in_=self.y[g * P:(g + 1) * P, :]),
                 writes=["xin%d" % s2], dma="xin%d" % s2)
            kbs = [kb for kb in (j - 1, j, j + 1) if 0 <= kb < nb]
            nk = len(kbs)
            gstate = {}

            def qk_part(hd):
                grp, hi = hd // 4, hd % 4
                c, hh = hd // 2, hd % 2
                kz = kTa if hh == 0 else kTb
                kzn = "kTa" if hh == 0 else "kTb"
                si = cnt["st"] % 2
                cnt["st"] += 1
                pi = cnt["pt"] % 3
                cnt["pt"] += 1
                for r, kb in enumerate(kbs):
                    s4 = (b0 + kb) % NR
                    single = (kb == j)
                    S.op("pe", (lambda e, r=r, s4=s4, single=single: e.matmul(
                        out=ST[si][:, r * P:(r + 1) * P], lhsT=kz[s4][:, grp, :], rhs=qT[s2][:, c, :],
                        start=True, stop=single)),
                        reads=["%s%d" % (kzn, s4), "qT%d" % s2], writes=["ST%d" % si])
                    if not single:
                        mk = self.maskL if kb < j else self.maskR
                        S.op("pe", (lambda e, r=r, mk=mk: e.matmul(
                            out=ST[si][:, r * P:(r + 1) * P], lhsT=self.ident, rhs=mk,
                            start=False, stop=True)),
                            reads=["ident", "maskL", "maskR"], writes=["ST%d" % si])
                S.op("act", lambda e: e.activation(out=PT[pi][:, 0:nk * P], in_=ST[si][:, 0:nk * P],
                                                   func=AF.Exp, scale=0.125),
                     reads=["ST%d" % si], writes=["PT%d" % pi])
                return pi

            def pv_part(hd, pi):
                grp, hi = hd // 4, hd % 4
                if hi == 0:
                    oi = cnt["ot"] % 2
                    cnt["ot"] += 1
                    gstate[grp] = oi
                oi = gstate[grp]
                OTv = OT[oi][:, 0:260].rearrange("p (h d) -> p h d", h=4)
                for r, kb in enumerate(kbs):
                    s4 = (b0 + kb) % NR
                    S.op("pe", (lambda e, r=r, s4=s4: e.matmul(
                        out=OTv[:, hi, :], lhsT=PT[pi][:, r * P:(r + 1) * P], rhs=va[s4][:, grp, :],
                        start=(r == 0), stop=(r == nk - 1))),
                        reads=["PT%d" % pi, "va%d" % s4], writes=["OT%d" % oi])
                if hi == 3:
                    dn = den[oi]
                    S.op("dve", lambda e: e.tensor_tensor(out=dn[:, 0:4], in0=OTv[:, :, 64], in1=esk[:, grp * 4:grp * 4 + 4],
                                                          op=ALU.add),
                         reads=["OT%d" % oi, "esk"], writes=["den%d" % oi])
                    S.op("dve", lambda e: e.reciprocal(out=dn[:, 4:8], in_=dn[:, 0:4]),
                         reads=["den%d" % oi], writes=["rden%d" % oi])
                    S.op("dve", lambda e: e.tensor_tensor(
                        out=otok[s2][:, grp * 256:(grp + 1) * 256].rearrange("p (h d) -> p h d", h=4),
                        in0=OTv[:, :, 0:64], in1=dn[:, 4:8].unsqueeze(2).broadcast_to([P, 4, 64]), op=ALU.mult),
                        reads=["OT%d" % oi, "rden%d" % oi], writes=["otok%d_%d" % (s2, grp)])

            pis = {0: qk_part(0)}
            for hd in range(16):
                if hd + 1 < 16:
                    pis[hd + 1] = qk_part(hd + 1)
                pv_part(hd, pis[hd])
            self.out_proj_block(g, otok[s2], ["otok%d_%d" % (s2, gq) for gq in range(4)], oT[s2], "oT%d" % s2, wo, xin[s2], "xin%d" % s2,
                                xo[s2], "xo%d" % s2, "xo%d" % s2)

        for si_, b0, nb in self.seq_blocks():
            load_blk(b0)
            for i in range(nb):
                if i + 1 < nb:
                    load_blk(b0 + i + 1)
                prep_blk(b0 + i)
                if i >= 1:
                    attn_blk(b0, nb, i - 1)
            attn_blk(b0, nb, nb - 1)
        S.barrier()
        self.free_stage()

    def hg_proj(self, li, ia):
        nc, S = self.nc, self.S
        NB = self.NB
        stg = [self.alloc([8192], BF16) for i in range(2)]
        lbb = [self.alloc([D], F32) for i in range(2)]
        omlb = [self.alloc([D], F32) for i in range(2)]
        q_sbs = [self.alloc([D], F32) for i in range(2)]
        q_sb = q_sbs[0]
        fl = [self.alloc([D], F32) for i in range(2)]
        kk = [self.alloc([D], F32) for i in range(2)]
        E1 = self.alloc([D], F32)
        E2 = self.alloc([D], F32)
        qe_toks = [self.alloc([D], BF16) for i in range(2)]
        sg_tok = self.alloc([D], BF16)
        sgf1 = self.alloc([512], F32)
        sgf = [sgf1, sgf1]
        gnb = self.alloc([D], F32)
        Mm = [self.alloc([P], F32) for i in range(2)]
        Msel = self.alloc([8], F32)
        ar_sb = [self.alloc([64], F32) for i in range(2)]
        PSS = [self.bank[6], self.bank[7]]
        PAR = self.bank[5]

        def setup():
            self.load_bcast(gnb, self.hg_g_norm[ia:ia + 1, :], "gnb")
            S.op("sp", lambda e: e.dma_start(out=Mm[0], in_=self.consts[:, 704:832]), writes=["Mm0"], dma="cst")
            S.op("sp", lambda e: e.dma_start(out=Mm[1], in_=self.consts[:, 832:960]), writes=["Mm1"], dma="cst")
            S.op("sp", lambda e: e.dma_start(out=Msel, in_=self.consts[:, 960:968]), writes=["Msel"], dma="cst")
            tl = [E1, E2, q_sb, fl[0]]
            for d in range(2):
                def one(d=d):
                    for l in range(4):
                        S.op("sp", (lambda e, l=l: e.dma_start(
                            out=tl[l], in_=self.hg_lb[d, l:l + 1, :].broadcast_to([P, D]))),
                            writes=["tl%d" % l], dma="cst")
                    mx = kk[0]
                    S.op("dve", lambda e: e.tensor_tensor(out=mx, in0=tl[0], in1=tl[1], op=ALU.max),
                         reads=["tl0", "tl1"], writes=["mx"])
                    S.op("dve", lambda e: e.tensor_tensor(out=mx, in0=mx, in1=tl[2], op=ALU.max),
                         reads=["mx", "tl2"], writes=["mx"])
                    S.op("dve", lambda e: e.tensor_tensor(out=mx, in0=mx, in1=tl[3], op=ALU.max),
                         reads=["mx", "tl3"], writes=["mx"])
                    for l in range(4):
                        S.op("dve", (lambda e, l=l: e.tensor_tensor(out=tl[l], in0=tl[l], in1=mx, op=ALU.subtract)),
                             reads=["mx", "tl%d" % l], writes=["tl%d" % l])
                        S.op("act", (lambda e, l=l: e.activation(out=tl[l], in_=tl[l], func=AF.Exp)),
                             reads=["tl%d" % l], writes=["tl%d" % l])
                    sm = kk[1]
                    S.op("dve", lambda e: e.tensor_tensor(out=sm, in0=tl[0], in1=tl[1], op=ALU.add),
                         reads=["tl0", "tl1"], writes=["sm"])
                    S.op("dve", lambda e: e.tensor_tensor(out=sm, in0=sm, in1=tl[2], op=ALU.add),
                         reads=["sm", "tl2"], writes=["sm"])
                    S.op("dve", lambda e: e.tensor_tensor(out=sm, in0=sm, in1=tl[3], op=ALU.add),
                         reads=["sm", "tl3"], writes=["sm"])
                    S.op("dve", lambda e: e.reciprocal(out=sm, in_=sm), reads=["sm"], writes=["sm"])
                    S.op("pool", lambda e: e.memset(lbb[d], 0.0), writes=["lbb%d" % d])
                    for l in range(1, li + 1):
                        S.op("dve", (lambda e, l=l: e.tensor_tensor(out=lbb[d], in0=lbb[d], in1=tl[l], op=ALU.add)),
                             reads=["lbb%d" % d, "tl%d" % l], writes=["lbb%d" % d])
                    S.op("dve", lambda e: e.tensor_tensor(out=lbb[d], in0=lbb[d], in1=sm, op=ALU.mult),
                         reads=["lbb%d" % d, "sm"], writes=["lbb%d" % d])
                    S.op("dve", lambda e: e.tensor_scalar(out=omlb[d], in0=lbb[d], scalar1=-1.0, scalar2=1.0,
                                                          op0=ALU.mult, op1=ALU.add),
                         reads=["lbb%d" % d], writes=["omlb%d" % d])
                one()
            for nms, t in ((["E1_0", "E1_1", "E1"], E1), (["E2_0", "E2_1", "E2"], E2), (["q_sb0"], q_sb),
                           (["fl0"], fl[0]), (["kk0"], kk[0]), (["kk1"], kk[1])):
                S.op("pool", (lambda e, t=t: e.tensor_copy(out=t[:, 0:1], in_=t[:, 0:1])),
                     reads=["tl0", "tl1", "tl2", "tl3", "mx", "sm", "omlb0", "omlb1"], writes=nms)

        d_sg, d_scan, d_scan2, d_tr = [], [], [], []

        def flushq(q):
            while q:
                q.pop(0)()

        def flush_all():
            flushq(d_sg)
            flushq(d_scan)
            flushq(d_scan2)
            flushq(d_tr)

        def transposes(src, ksrc, pti, dst, kdst, eng):
            for c in range(8):
                S.op("pe", (lambda e, c=c: e.transpose(out=self.pt[pti][:, c * P:(c + 1) * P],
                                                       in_=src[:, c * P:(c + 1) * P], identity=self.ident)),
                     reads=[ksrc, "ident"], writes=["pt%d" % pti])
            if eng == "act":
                S.op("act", lambda e: e.copy(out=dst, in_=self.pt[pti]), reads=["pt%d" % pti], writes=[kdst])
            else:
                S.op("dve", lambda e: e.tensor_copy(out=dst, in_=self.pt[pti]), reads=["pt%d" % pti], writes=[kdst])

        def gates_dir(b, d):
            s2 = b % 2
            st = stg[s2]
            S.op("pool", lambda e: e.tensor_scalar(out=kk[d], in0=fl[d], scalar1=-1.0, scalar2=1.0,
                                                   op0=ALU.mult, op1=ALU.add),
                 reads=["fl%d" % d], writes=["kk%d" % d])
            S.op("act", lambda e: e.activation(out=fl[d], in_=fl[d], func=AF.Ln),
                 reads=["fl%d" % d, "kk%d" % d], writes=["fl%d" % d])

        def scan_dir(b, d):
            s2 = b % 2
            st = stg[s2]
            for hf in range(2):
                S.op("pe", (lambda e, hf=hf: e.matmul(out=PSS[hf], lhsT=Mm[d], rhs=fl[d][:, hf * 512:(hf + 1) * 512],
                                                     start=True, stop=True)),
                     reads=["Mm%d" % d, "fl%d" % d], writes=["PSS%d" % hf])
            for hd in range(8):
                S.op("pe", (lambda e, hd=hd: e.matmul(out=PAR[:, hd * 4:hd * 4 + 4], lhsT=fl[d][:, hd * P:(hd + 1) * P],
                                                     rhs=Msel[:, d * 4:d * 4 + 4], start=True, stop=True)),
                     reads=["fl%d" % d, "Msel"], writes=["pt1"])
            S.op("act", lambda e: e.activation(out=ar_sb[s2][:, d * 32:(d + 1) * 32], in_=PAR[:, 0:32], func=AF.Exp),
                 reads=["pt1"], writes=["ar%d_%d" % (s2, d)])
            for hf in range(2):
                S.op("act", (lambda e, hf=hf: e.activation(out=E1[:, hf * 512:(hf + 1) * 512], in_=PSS[hf], func=AF.Exp)),
                     reads=["PSS%d" % hf], writes=["E1_%d" % hf])
                S.op("act", (lambda e, hf=hf: e.activation(out=E2[:, hf * 512:(hf + 1) * 512], in_=PSS[hf], func=AF.Exp,
                                                           scale=-1.0)),
                     reads=["PSS%d" % hf], writes=["E2_%d" % hf])
            qe_tok = qe_toks[d]
            S.op("pool", lambda e: e.tensor_tensor(out=qe_tok, in0=q_sbs[s2], in1=E1, op=ALU.mult),
                 reads=["q_sb%d" % s2, "E1_0", "E1_1", "E1"], writes=["qe_tok%d" % d, "E1"])
            ke = st[:, 1024 + d * 1024:2048 + d * 1024]
            S.op("dve", lambda e: e.tensor_tensor(out=ke, in0=kk[d], in1=E2, op=ALU.mult),
                 reads=["kk%d" % d, "E2_0", "E2_1", "E2"], writes=["st%d_ke%d" % (s2, d), "E2"])

            def tr():
                transposes(qe_tok, "qe_tok%d" % d, 0, st[:, 3072 + d * 2048:4096 + d * 2048], "st%d_qeT%d" % (s2, d), "act")
                transposes(ke, "st%d_ke%d" % (s2, d), 1, st[:, 4096 + d * 2048:5120 + d * 2048],
                           "st%d_keT%d" % (s2, d), "dve")
            d_tr.append(tr)

        def store(b):
            s2 = b % 2
            keys = ["st%d_v0" % s2, "st%d_v1" % s2, "st%d_sgT" % s2]
            for d in range(2):
                keys += ["st%d_ke%d" % (s2, d), "st%d_qeT%d" % (s2, d), "st%d_keT%d" % (s2, d)]
            S.op("sp", lambda e: e.dma_start(out=self.hg_scr[b], in_=stg[s2]), reads=keys, dma="st%d" % s2)
            S.op("sp", lambda e: e.dma_start(out=self.ar_scr[b], in_=ar_sb[s2]),
                 reads=["ar%d_0" % s2, "ar%d_1" % s2], dma="ar%d" % s2)

        def evac(b, si, ps, kps):
            s2 = b % 2
            st = stg[s2]
            if si == 1:
                flushq(d_scan)
            if si == 3:
                flushq(d_scan2)
            if si == 7:
                flushq(d_tr)
            if si < 2:
                S.op("dve", lambda e: e.tensor_copy(out=q_sbs[s2][:, si * 512:(si + 1) * 512], in_=ps),
                     reads=[kps], writes=["q_sb%d" % s2])
            elif si < 4:
                S.op("dve", lambda e: e.tensor_copy(out=st[:, (si - 2) * 512:(si - 1) * 512], in_=ps),
                     reads=[kps], writes=["st%d_v%d" % (s2, si - 2)])
            elif si < 6:
                k = si - 4
                S.op("act", lambda e: e.activation(out=sgf[k], in_=ps, func=AF.Silu), reads=[kps], writes=["sgf"])
                S.op("dve", lambda e: e.tensor_tensor(out=sg_tok[:, k * 512:(k + 1) * 512], in0=sgf[k],
                                                      in1=gnb[:, k * 512:(k + 1) * 512], op=ALU.mult),
                     reads=["sgf", "gnb"], writes=["sg_tok"])
                if si == 5:
                    d_sg.append(lambda: transposes(sg_tok, "sg_tok", 1, st[:, 7168:8192], "st%d_sgT" % s2, "dve"))
            else:
                d = (si - 6) // 2
                k = (si - 6) % 2
                cs_ = slice(k * 512, (k + 1) * 512)
                S.op("act", lambda e: e.activation(out=fl[d][:, cs_], in_=ps, func=AF.Sigmoid),
                     reads=[kps], writes=["fl%d" % d])
                S.op("dve", lambda e: e.tensor_tensor(out=fl[d][:, cs_], in0=fl[d][:, cs_], in1=omlb[d][:, cs_], op=ALU.mult),
                     reads=["fl%d" % d, "omlb%d" % d], writes=["fl%d" % d])
                S.op("dve", lambda e: e.tensor_tensor(out=fl[d][:, cs_], in0=fl[d][:, cs_], in1=lbb[d][:, cs_], op=ALU.add),
                     reads=["fl%d" % d, "lbb%d" % d], writes=["fl%d" % d])
                if si == 7:
                    flushq(d_sg)
                if si == 9:
                    gates_dir(b, 0)
                    gates_dir(b, 1)

                    def sc():
                        scan_dir(b, 0)

                    def sc2():
                        scan_dir(b, 1)
                        d_tr.append(lambda: store(b))
                    d_scan.append(sc)
                    d_scan2.append(sc2)

        def post(b):
            pass

        self.proj_stage(li, self.hg_w_in[ia], 5 * D, evac, post, extra_setup=setup, finish=flush_all, nx=2)

    def hg_mix(self, li, ia):
        nc, S = self.nc, self.S
        wo = self.alloc([8, D], BF16)
        self.load_weight(wo, self.hg_w_o[ia], 8, "wo", "w1")
        NRI = 4
        vt = [self.alloc([D], BF16) for i in range(NRI)]
        kz0 = [self.alloc([D], BF16) for i in range(NRI)]
        kz1 = [self.alloc([D], BF16) for i in range(NRI)]
        qk = [self.alloc([2 * D], BF16) for i in range(NRI)]
        NX = 4
        sgT = [self.alloc([D], BF16) for i in range(NX)]
        obw = [self.alloc([D], F32) for i in range(NX)]
        xin = [self.alloc([D], F32) for i in range(NX)]
        xo = [self.alloc([D], F32) for i in range(2)]
        osum = [self.alloc([D], F32) for i in range(3)]
        sq = [self.alloc([D], BF16) for i in range(3)]
        ms = [self.alloc([D], F32) for i in range(3)]
        ogf = [self.alloc([8, P], BF16) for i in range(3)]
        Wst = [self.alloc([8, P], F32) for i in range(2)]
        epsb = self.alloc([4], F32)
        Sb = self.alloc([8, 5, P], BF16)
        Am = [self.alloc([8, P], BF16) for i in range(2)]
        Araw = [self.alloc([8, P], BF16) for i in range(2)]
        maxnb = max(L // P for L in self.seqs)
        ar = self.alloc([maxnb, 64], F32)
        cc = [self.alloc([maxnb, 8, 2], F32) for i in range(2)]
        mk = [self.alloc([P], BF16) for i in range(2)]
        ones = self.alloc([P], BF16)
        mstage = self.alloc([256], F32)
        S.op("sp", lambda e: e.dma_start(out=mstage, in_=self.consts[:, 1024:1280]), writes=["mstage"], dma="cst")
        for d in range(2):
            S.op("dve", (lambda e, d=d: e.tensor_copy(out=mk[d], in_=mstage[:, d * P:(d + 1) * P])),
                 reads=["mstage"], writes=["mk%d" % d])
        S.op("pool", lambda e: e.memset(ones, 1.0), writes=["ones"])
        S.op("pool", lambda e: e.memset(epsb, EPS), writes=["epsb"])
        for i in range(NRI):
            S.op("pool", (lambda e, i=i: e.memset(kz0[i], 0.0)), writes=["kz0_%d" % i])
            S.op("pool", (lambda e, i=i: e.memset(kz1[i], 0.0)), writes=["kz1_%d" % i])
        PA = [self.bank[0], self.bank[1]]
        PU = [self.bank[2], self.bank[3]]
        POB = [self.bank[4], self.bank[5]]
        O = [self.bank[6], self.bank[7]]
        cnt = {"tmp": 0, "ld": 0, "x": 0}

        def run_pass(b0, nb, d):
            order = [0, 1] if d == 0 else [1, 0]
            blocks = list(range(nb)) if d == 0 else list(range(nb - 1, -1, -1))
            c = cc[d]
            slot_of = {}

            def load_main(ii):
                if ii >= nb:
                    return
                g = b0 + blocks[ii]
                sl = cnt["ld"] % NRI
                cnt["ld"] += 1
                slot_of[ii] = sl
                src = self.hg_scr[g]
                S.op("sp", lambda e: e.dma_start(out=vt[sl], in_=src[:, 0:1024]), writes=["vt%d" % sl], dma="hv%d" % sl)
                kc0 = 1024 + d * 1024
                S.op("sp", lambda e: e.dma_start(out=kz0[sl][0:64, :], in_=src[0:64, kc0:kc0 + 1024]),
                     writes=["kz0_%d" % sl], dma="hk%d" % sl)
                S.op("sp", lambda e: e.dma_start(out=kz1[sl][64:128, :], in_=src[64:128, kc0:kc0 + 1024]),
                     writes=["kz1_%d" % sl], dma="hj%d" % sl)
                qc0 = 3072 + d * 2048
                S.op("sp", lambda e: e.dma_start(out=qk[sl], in_=src[:, qc0:qc0 + 2048]), writes=["qk%d" % sl],
                     dma="hq%d" % sl)

            xslot = {}

            def load_extra(ii):
                if ii >= nb or d == 1:
                    return
                g = b0 + blocks[ii]
                sx = cnt["x"] % NX
                cnt["x"] += 1
                xslot[ii] = sx
                S.op("sp", lambda e: e.dma_start(out=sgT[sx], in_=self.hg_scr[g][:, 7168:8192]), writes=["sgT%d" % sx],
                     dma="hx%d" % sx)
                S.op("sp", lambda e: e.dma_start(out=obw[sx], in_=self.obw_scr[g]), reads=["obwd%d" % g],
                     writes=["obw%d" % sx], dma="hx%d" % sx)
                S.op("sp", lambda e: e.dma_start(out=xin[sx], in_=self.y[g * P:(g + 1) * P, :]), writes=["xin%d" % sx],
                     dma="hx%d" % sx)

            S.op("sp", lambda e: e.dma_start(out=ar[:, 0:nb, :], in_=self.ar_scr[b0:b0 + nb].rearrange("b p c -> p b c")),
                 writes=["ar"], dma="cst")
            arv = ar.rearrange("p b (d h j) -> p b d h j", d=2, h=8)
            S.op("pool", lambda e: e.memset(c, 1.0), writes=["cc%d" % d])
            if d == 0:
                S.op("dve", lambda e: e.tensor_tensor(out=c[:, 0:nb, :, 0], in0=arv[:, 0:nb, 0, :, 0],
                                                      in1=arv[:, 0:nb, 0, :, 3], op=ALU.mult),
                     reads=["ar", "cc%d" % d], writes=["cc%d" % d])
                S.op("dve", lambda e: e.tensor_tensor(out=c[:, 0:nb - 1, :, 1], in0=arv[:, 0:nb - 1, 0, :, 2],
                                                      in1=arv[:, 1:nb, 0, :, 1], op=ALU.mult),
                     reads=["ar", "cc%d" % d], writes=["cc%d" % d])
            else:
                S.op("dve", lambda e: e.tensor_tensor(out=c[:, 0:nb, :, 1], in0=arv[:, 0:nb, 1, :, 2],
                                                      in1=arv[:, 0:nb, 1, :, 1], op=ALU.mult),
                     reads=["ar", "cc%d" % d], writes=["cc%d" % d])
                S.op("dve", lambda e: e.tensor_tensor(out=c[:, 1:nb, :, 0], in0=arv[:, 1:nb, 1, :, 0],
                                                      in1=arv[:, 0:nb - 1, 1, :, 3], op=ALU.mult),
                     reads=["ar", "cc%d" % d], writes=["cc%d" % d])
            S.op("pool", lambda e: e.memset(Wst[1], 0.0), writes=["W1_%d" % h_ for h_ in range(8)])
            S.op("pool", lambda e: e.memset(Sb[:, :, 0, :], 0.0), writes=["Sb%d_0" % h_ for h_ in range(8)])

            def P1(ii):
                blk = blocks[ii]
                sl = slot_of[ii]
                par = ii % 2
                last_block = (ii == nb - 1)

                def head_mm(h):
                    hs = slice(h * P, (h + 1) * P)
                    S.op("pe", lambda e: e.matmul(out=PA[h % 2][:, 0:P], lhsT=qk[sl][:, D + h * P:D + (h + 1) * P],
                                                  rhs=qk[sl][:, hs], start=True, stop=True),
                         reads=["qk%d" % sl], writes=["PA%d" % (h % 2)])
                    S.op("act", lambda e: e.copy(out=Araw[par][:, h, :], in_=PA[h % 2][:, 0:P]),
                         reads=["PA%d" % (h % 2)], writes=["Ar%d_%d" % (par, h)])
                    S.op("pool", lambda e: e.tensor_tensor(out=Am[par][:, h, :], in0=Araw[par][:, h, :], in1=mk[d], op=ALU.mult),
                         reads=["Ar%d_%d" % (par, h), "mk%d" % d], writes=["Am%d_%d" % (par, h)])
                    S.op("pe", lambda e: e.matmul(out=PU[h % 2][:, 0:P], lhsT=kz0[sl][:, hs], rhs=vt[sl][:, hs],
                                                  start=True, stop=True),
                         reads=["kz0_%d" % sl, "vt%d" % sl], writes=["PU%d" % (h % 2)])
                    S.op("pe", lambda e: e.matmul(out=PU[h % 2][:, P:2 * P], lhsT=kz1[sl][:, hs], rhs=vt[sl][:, hs],
                                                  start=True, stop=True),
                         reads=["kz1_%d" % sl, "vt%d" % sl], writes=["PU%d" % (h % 2)])

                def head_step(h, j):
                    if last_block and j == 1:
                        return
                    p = order[j]
                    k_out = 2 * ii + j + 1
                    cs_ = c[:, blk, h, p:p + 1]
                    if j == 1:
                        cprev = c[:, blk, h, order[0]:order[0] + 1]
                    elif ii == 0:
                        cprev = 1.0
                    else:
                        pb_ = blocks[ii - 1]
                        cprev = c[:, pb_, h, order[1]:order[1] + 1]
                    wo_, wi_ = (k_out - 1) % 2, k_out % 2
                    S.op("dve", lambda e: e.scalar_tensor_tensor(
                        out=Wst[wo_][:, h, :], in0=Wst[wi_][:, h, :], scalar=cprev,
                        in1=PU[h % 2][:, p * P:(p + 1) * P], op0=ALU.mult, op1=ALU.add),
                        reads=["PU%d" % (h % 2), "W%d_%d" % (wi_, h), "cc%d" % d], writes=["W%d_%d" % (wo_, h)])
                    S.op("act", lambda e: e.activation(out=Sb[:, h, k_out % 5, :], in_=Wst[wo_][:, h, :], func=AF.Copy,
                                                       scale=cs_),
                         reads=["W%d_%d" % (wo_, h), "cc%d" % d], writes=["Sb%d_%d" % (h, k_out % 5)])

                for hp in range(4):
                    head_mm(2 * hp)
                    head_mm(2 * hp + 1)
                    for j in range(2):
                        head_step(2 * hp, j)
                        head_step(2 * hp + 1, j)

            def P2(ii):
                sl = slot_of[ii]
                par = ii % 2

                def head(h):
                    bk = h // 4
                    c0 = (h % 4) * P
                    hs = slice(h * P, (h + 1) * P)
                    S.op("pe", lambda e: e.matmul(out=POB[bk][:, c0:c0 + P], lhsT=vt[sl][:, hs], rhs=Am[par][:, h, :],
                                                  start=True, stop=False),
                         reads=["vt%d" % sl, "Am%d_%d" % (par, h)], writes=["POB%d" % bk])
                    for j in range(2):
                        p = order[j]
                        k_in = 2 * ii + j
                        S.op("pe", (lambda e, p=p, k_in=k_in, j=j: e.matmul(
                            out=POB[bk][:, c0 + p * 64:c0 + (p + 1) * 64], lhsT=Sb[:, h, k_in % 5, :],
                            rhs=qk[sl][:, h * P + p * 64:h * P + (p + 1) * 64], start=False, stop=(j == 1))),
                            reads=["Sb%d_%d" % (h, k_in % 5), "qk%d" % sl], writes=["POB%d" % bk])
                for h in range(8):
                    head(h)

            def evac_bwd(ii):
                g = b0 + blocks[ii]
                so = ii % 2
                for bk in range(2):
                    S.op("act", (lambda e, bk=bk: e.copy(out=obw[so][:, bk * 512:(bk + 1) * 512], in_=POB[bk])),
                         reads=["POB%d" % bk], writes=["obw%d" % so])
                S.op("sp", lambda e: e.dma_start(out=self.obw_scr[g], in_=obw[so]),
                     reads=["obw%d" % so], writes=["obwd%d" % g], dma="hx%d" % so)

            def tail0(ii):
                sx = xslot[ii]
                s3 = ii % 3
                for bk in range(2):
                    S.op("dve", (lambda e, bk=bk: e.tensor_tensor(out=osum[s3][:, bk * 512:(bk + 1) * 512], in0=POB[bk],
                                                                  in1=obw[sx][:, bk * 512:(bk + 1) * 512], op=ALU.add)),
                         reads=["POB%d" % bk, "obw%d" % sx], writes=["osum%d_%d" % (s3, bk)])
                S.op("act", lambda e: e.activation(out=sq[s3], in_=osum[s3], func=AF.Square),
                     reads=["osum%d_0" % s3, "osum%d_1" % s3], writes=["sq%d" % s3])

            def tailA(ii):
                sx = xslot[ii]
                s3 = ii % 3
                for hf in range(2):
                    S.op("pe", (lambda e, hf=hf: e.matmul(out=O[hf], lhsT=ones, rhs=sq[s3][:, hf * 512:(hf + 1) * 512],
                                                         start=True, stop=True)),
                         reads=["ones", "sq%d" % s3], writes=["O%d" % hf])
                    S.op("act", (lambda e, hf=hf: e.activation(out=ms[s3][:, hf * 512:(hf + 1) * 512], in_=O[hf],
                                                               func=AF.Ln, scale=1.0 / 128.0, bias=epsb[:, 0:1])),
                         reads=["O%d" % hf, "epsb"], writes=["ms%d_%d" % (s3, hf)])
                    S.op("act", (lambda e, hf=hf: e.activation(out=ms[s3][:, hf * 512:(hf + 1) * 512],
                                                               in_=ms[s3][:, hf * 512:(hf + 1) * 512],
                                                               func=AF.Exp, scale=-0.5)),
                         reads=["ms%d_%d" % (s3, hf)], writes=["ms%d_%d" % (s3, hf)])
                mk_ = ["ms%d_0" % s3, "ms%d_1" % s3]
                ok_ = ["osum%d_0" % s3, "osum%d_1" % s3]
                S.op("dve", lambda e: e.tensor_tensor(out=osum[s3], in0=osum[s3], in1=ms[s3], op=ALU.mult),
                     reads=ok_ + mk_, writes=ok_)
                S.op("pool", lambda e: e.tensor_tensor(out=ogf[s3].rearrange("p a b -> p (a b)"), in0=osum[s3], in1=sgT[sx],
                                                       op=ALU.mult),
                     reads=ok_ + ["sgT%d" % sx], writes=["ogf%d" % s3])

            def tailB(ii):
                g = b0 + blocks[ii]
                sx = xslot[ii]
                s3 = ii % 3
                s2 = ii % 2
                for kc in range(8):
                    for hf in range(2):
                        S.op("pe", (lambda e, kc=kc, hf=hf: e.matmul(
                            out=O[hf], lhsT=ogf[s3][:, kc, :], rhs=wo[:, kc, hf * 512:(hf + 1) * 512],
                            start=(kc == 0), stop=(kc == 7))),
                            reads=["ogf%d" % s3, "wo"], writes=["O%d" % hf])
                for hf in range(2):
                    S.op("dve", (lambda e, hf=hf: e.tensor_tensor(
                        out=xo[s2][:, hf * 512:(hf + 1) * 512], in0=O[hf], in1=xin[sx][:, hf * 512:(hf + 1) * 512],
                        op=ALU.add)),
                        reads=["O%d" % hf, "xin%d" % sx], writes=["xo%d_%d" % (s2, hf)])
                S.op("sp", lambda e: e.dma_start(out=self.y[g * P:(g + 1) * P, :], in_=xo[s2]),
                     reads=["xo%d_0" % s2, "xo%d_1" % s2], dma="xo%d" % s2)

            load_main(0)
            load_main(1)
            load_main(2)
            load_extra(0)
            P1(0)
            for ii in range(nb):
                load_main(ii + 3)
                if ii + 1 < nb:
                    P1(ii + 1)
                P2(ii)
                if d == 1:
                    evac_bwd(ii)
                else:
                    tail0(ii)
                    if ii >= 1:
                        tailA(ii - 1)
                    if ii >= 2:
                        tailB(ii - 2)
                    load_extra(ii + 1)
            if d == 0:
                tailA(nb - 1)
                if nb >= 2:
                    tailB(nb - 2)
                tailB(nb - 1)

        for si_, b0, nb in self.seq_blocks():
            run_pass(b0, nb, 1)
            run_pass(b0, nb, 0)
        S.barrier()
        self.free_stage()

    def na_proj(self, li):
        S = self.S
        st = [self.alloc([3072], BF16) for i in range(2)]

        def evac(b, si, ps, kps):
            s2 = b % 2
            dst = st[s2][:, si * 512:(si + 1) * 512]
            if si % 2 == 0:
                S.op("act", lambda e: e.copy(out=dst, in_=ps), reads=[kps], writes=["st%d_%d" % (s2, si)])
            else:
                S.op("dve", lambda e: e.tensor_copy(out=dst, in_=ps), reads=[kps], writes=["st%d_%d" % (s2, si)])

        def post(b):
            s2 = b % 2
            S.op("sp", lambda e: e.dma_start(out=self.qkv_scr[b * P:(b + 1) * P, :], in_=st[s2]),
                 reads=["st%d_%d" % (s2, k) for k in range(6)], dma="st%d" % s2)

        self.proj_stage(li, self.na_w_qkv[0], 3072, evac, post)

    def na_mix(self, li):
        nc, S = self.nc, self.S
        wo = self.alloc([8, D], BF16)
        self.load_weight(wo, self.na_w_o[0], 8, "wo", "w1")
        NR = 6
        qkv = [self.alloc([3072], BF16) for i in range(2)]
        qT = [self.alloc([8, P], BF16) for i in range(4)]
        kTa = [self.alloc([8, P], BF16) for i in range(NR)]
        kTb = [self.alloc([8, P], BF16) for i in range(NR)]
        va = [self.alloc([16, 65], BF16) for i in range(NR)]
        PT = [self.alloc([640], BF16) for i in range(3)]
        otok = [self.alloc([D], BF16) for i in range(2)]
        oT = [self.alloc([8, P], BF16) for i in range(2)]
        xin = [self.alloc([D], F32) for i in range(2)]
        xo = [self.alloc([D], F32) for i in range(2)]
        den = [self.alloc([4], F32) for i in range(3)]
        T2 = self.alloc([16, 15, 64], BF16)
        stg = [self.alloc([15, 64], F32) for i in range(2)]
        cmask = self.alloc([64], F32)
        S.op("sp", lambda e: e.dma_start(out=cmask, in_=self.consts[:, 640:704]), writes=["cmask"], dma="cst")
        for i in range(2):
            S.op("pool", (lambda e, i=i: e.memset(stg[i], 0.0)), writes=["stg%d" % i, "stg%db" % i])
        for hd in range(16):
            i = hd % 2
            S.op("sp", (lambda e, hd=hd, i=i: e.dma_start(out=stg[i][0:64, :, :],
                                                           in_=self.na_bias[hd].rearrange("j k q -> k j q"))),
                 writes=["stg%d" % i], dma="stg%da" % i)
            S.op("sp", (lambda e, hd=hd, i=i: e.dma_start(out=stg[i][64:128, 1:15, :],
                                                           in_=self.na_bias[hd][0:14].rearrange("j k q -> k j q"))),
                 writes=["stg%db" % i], dma="stg%db" % i)
            S.op("dve", (lambda e, hd=hd, i=i: e.scalar_tensor_tensor(
                out=T2[:, hd, :, :], in0=stg[i], scalar=8.0, in1=cmask.unsqueeze(1).broadcast_to([P, 15, 64]),
                op0=ALU.mult, op1=ALU.add)),
                reads=["stg%d" % i, "stg%db" % i, "cmask"], writes=["T2", "stg%d" % i, "stg%db" % i])
        for i in range(NR):
            S.op("pool", (lambda e, i=i: e.memset(kTa[i], 0.0)), writes=["kTa%d" % i])
            S.op("pool", (lambda e, i=i: e.memset(kTb[i], 0.0)), writes=["kTb%d" % i])
            S.op("pool", (lambda e, i=i: e.memset(va[i], 1.0)), writes=["va%d" % i])
        ST = [self.psum[:, 0:640], self.psum[:, 1024:1664]]
        OT1 = self.bank[4][:, 0:130].rearrange("p (h d) -> p h d", h=2)
        OT = [OT1, OT1, OT1]
        cnt = {"st": 0, "pt": 0, "ot": 0}
        rowmasks = {}

        def get_rowmask(pat):
            if pat not in rowmasks:
                t = self.alloc([P], BF16)
                nm = "rm%d" % len(rowmasks)
                for kr in range(2):
                    for qr in range(2):
                        val = 0.0 if pat[kr * 2 + qr] else NEG
                        S.op("pool", (lambda e, kr=kr, qr=qr, val=val: e.memset(
                            t[kr * 64:(kr + 1) * 64, qr * 64:(qr + 1) * 64], val)), writes=[nm])
                rowmasks[pat] = (t, nm)
            return rowmasks[pat]

        def load_blk(g):
            s2 = g % 2
            S.op("sp", lambda e: e.dma_start(out=qkv[s2], in_=self.qkv_scr[g * P:(g + 1) * P, :]),
                 writes=["qkv%d" % s2], dma="qkv%d" % s2)

        def prep_blk(g):
            s2, s4, s6 = g % 2, g % 4, g % NR
            for c in range(8):
                S.op("pe", (lambda e, c=c: e.transpose(out=self.pt[1][:, c * P:(c + 1) * P],
                                                       in_=qkv[s2][:, c * P:(c + 1) * P], identity=self.ident)),
                     reads=["qkv%d" % s2, "ident"], writes=["pt1"])
            S.op("act", lambda e: e.copy(out=qT[s4].rearrange("p a b -> p (a b)"), in_=self.pt[1]),
                 reads=["pt1"], writes=["qT%d" % s4])
            for c in range(8):
                S.op("pe", (lambda e, c=c: e.transpose(out=self.pt[1][:, c * P:(c + 1) * P],
                                                       in_=qkv[s2][:, 1024 + c * P:1024 + (c + 1) * P],
                                                       identity=self.ident)),
                     reads=["qkv%d" % s2, "ident"], writes=["pt1"])
            pv = self.pt[1].rearrange("p (a b) -> p a b", a=8)
            S.op("act", lambda e: e.copy(out=kTa[s6][0:64], in_=pv[0:64]), reads=["pt1"], writes=["kTa%d" % s6])
            S.op("dve", lambda e: e.tensor_copy(out=kTb[s6][64:128], in_=pv[64:128]), reads=["pt1"],
                 writes=["kTb%d" % s6])
            S.op("pool", lambda e: e.tensor_copy(out=va[s6][:, :, 0:64],
                                                 in_=qkv[s2][:, 2048:3072].rearrange("p (h d) -> p h d", h=16)),
                 reads=["qkv%d" % s2], writes=["va%d" % s6])

        def needed(nb, m):
            R = 2 * nb
            rs = [min(max(2 * m + qr - 4, 0), R - 8) for qr in range(2)]
            kbs = []
            for b in range(nb):
                pat = tuple(1 if rs[qr] <= 2 * b + kr < rs[qr] + 8 else 0 for kr in range(2) for qr in range(2))
                if any(pat):
                    kbs.append((b, pat))
            return kbs

        def attn_blk(b0, nb, m):
            g = b0 + m
            s2 = g % 2
            s4q = g % 4
            S.op("sp", lambda e: e.dma_start(out=xin[s2], in_=self.y[g * P:(g + 1) * P, :]),
                 writes=["xin%d" % s2], dma="xin%d" % s2)
            kbs = needed(nb, m)
            nk = len(kbs)
            assert nk <= 5
            def qk_part(hd):
                pr, hh = hd // 2, hd % 2
                kz = kTa if hh == 0 else kTb
                kzn = "kTa" if hh == 0 else "kTb"
                si = cnt["st"] % 2
                cnt["st"] += 1
                pi = cnt["pt"] % 3
                cnt["pt"] += 1
                for r, (kb, pat) in enumerate(kbs):
                    s6 = (b0 + kb) % NR
                    Dd = 2 * (kb - m)
                    bias = T2[:, hd, 7 - Dd:9 - Dd, :].rearrange("p a b -> p (a b)")
                    full = all(pat)
                    S.op("pe", (lambda e, r=r, s6=s6: e.matmul(
                        out=ST[si][:, r * P:(r + 1) * P], lhsT=kz[s6][:, pr, :], rhs=qT[s4q][:, pr, :],
                        start=True, stop=False)),
                        reads=["%s%d" % (kzn, s6), "qT%d" % s4q], writes=["ST%d" % si])
                    S.op("pe", (lambda e, r=r, bias=bias, full=full: e.matmul(
                        out=ST[si][:, r * P:(r + 1) * P], lhsT=self.ident, rhs=bias,
                        start=False, stop=full)),
                        reads=["ident", "T2"], writes=["ST%d" % si])
                    if not full:
                        rm, rmn = get_rowmask(pat)
                        S.op("pe", (lambda e, r=r, rm=rm: e.matmul(
                            out=ST[si][:, r * P:(r + 1) * P], lhsT=self.ident, rhs=rm,
                            start=False, stop=True)),
                            reads=["ident", rmn], writes=["ST%d" % si])
                S.op("act", lambda e: e.activation(out=PT[pi][:, 0:nk * P], in_=ST[si][:, 0:nk * P],
                                                   func=AF.Exp, scale=0.125),
                     reads=["ST%d" % si], writes=["PT%d" % pi])
                return pi

            def pv_part(hd, pi):
                pr, hh = hd // 2, hd % 2
                oi = 0
                for r, (kb, pat) in enumerate(kbs):
                    s6 = (b0 + kb) % NR
                    S.op("pe", (lambda e, r=r, s6=s6: e.matmul(
                        out=OT[oi][:, hh, :], lhsT=PT[pi][:, r * P:(r + 1) * P], rhs=va[s6][:, hd, :],
                        start=(r == 0), stop=(r == nk - 1))),
                        reads=["PT%d" % pi, "va%d" % s6], writes=["OT"])
                if hh == 1:
                    dn = den[pr % 3]
                    S.op("dve", lambda e: e.reciprocal(out=dn[:, 0:2], in_=OT[oi][:, :, 64]),
                         reads=["OT"], writes=["rden%d" % (pr % 3)])
                    S.op("dve", lambda e: e.tensor_tensor(
                        out=otok[s2][:, pr * 128:(pr + 1) * 128].rearrange("p (h d) -> p h d", h=2),
                        in0=OT[oi][:, :, 0:64], in1=dn[:, 0:2].unsqueeze(2).broadcast_to([P, 2, 64]), op=ALU.mult),
                        reads=["OT", "rden%d" % (pr % 3)], writes=["otok%d_%d" % (s2, pr)])

            pis = {0: qk_part(0)}
            for hd in range(16):
                if hd + 1 < 16:
                    pis[hd + 1] = qk_part(hd + 1)
                pv_part(hd, pis[hd])
            self.out_proj_block(g, otok[s2], ["otok%d_%d" % (s2, gq) for gq in range(8)], oT[s2], "oT%d" % s2, wo, xin[s2], "xin%d" % s2,
                                xo[s2], "xo%d" % s2, "xo%d" % s2, pti=1)

        for si_, b0, nb in self.seq_blocks():
            pending = list(range(nb))
            load_blk(b0)
            for i in range(nb):
                if i + 1 < nb:
                    load_blk(b0 + i + 1)
                for m in pending:
                    for kb, _ in needed(nb, m):
                        assert kb >= i or ((b0 + kb) % NR) != ((b0 + i) % NR), "NA ring too small"
                prep_blk(b0 + i)
                while pending and max(kb for kb, _ in needed(nb, pending[0])) <= i:
                    attn_blk(b0, nb, pending.pop(0))
            assert not pending
        S.barrier()
        self.free_stage()


def build(seqs, stages, dbg=None):
    B = Builder(seqs, dbg=dbg)
    src = B.x
    for st in stages:
        if st[0] == "ffn":
            B.S.new_epoch()
            B.ffn_stage(st[1], st[2], src, final_norm=(len(st) > 3 and st[3]))
            src = B.y
        elif st[0] == "swa":
            B.swa_proj(st[1])
            B.swa_mix(st[1])
        elif st[0] == "hg_proj":
            B.hg_proj(st[1], st[2])
        elif st[0] == "hg":
            B.hg_proj(st[1], st[2])
            B.hg_mix(st[1], st[2])
        elif st[0] == "swa_proj":
            B.swa_proj(st[1])
        elif st[0] == "na_proj":
            B.na_proj(st[1])
        elif st[0] == "na":
            B.na_proj(st[1])
            B.na_mix(st[1])
    B.S.emit()
    B.n_sems = len(B.S.chan_sem)
    return B.nc


def na_bias_layout(na_rpb):
    r = np.asarray(na_rpb, dtype=np.float32)[0]
    kc = np.arange(64)[:, None]
    qc = np.arange(64)[None, :]
    dc = np.clip(kc - qc + 15, 0, 30)
    e = r[:, ::-1, :][:, :, dc]
    return np.ascontiguousarray(e)


def const_tables():
    c = np.zeros((P, 2048), np.float32)
    c[:, 0:128] = np.eye(P, dtype=np.float32)
    kk = np.arange(P)[:, None]
    qq = np.arange(P)[None, :]
    c[:, 128:256] = np.where(kk >= qq, 0.0, NEG)
    c[:, 256:384] = np.where(kk <= qq, 0.0, NEG)
    kc = (np.arange(P) % 64)[:, None]
    qc = np.arange(64)[None, :]
    cs = np.clip(qc - 8, 0, 48)
    c[:, 640:704] = np.where((kc >= cs) & (kc < cs + 16), 0.0, NEG)
    sv = np.arange(P)[:, None]
    tv = np.arange(P)[None, :]
    same = (sv // 64) == (tv // 64)
    sl = sv % 64
    c[:, 704:832] = np.where(same, (sv <= tv).astype(np.float32) - (sl <= 31).astype(np.float32), 0.0)
    c[:, 832:960] = np.where(same, (sv >= tv).astype(np.float32) - (sl >= 32).astype(np.float32), 0.0)
    s1 = np.arange(P)
    c[:, 960] = ((s1 >= 32) & (s1 <= 63))
    c[:, 961] = (s1 <= 31)
    c[:, 962] = (s1 >= 96)
    c[:, 963] = ((s1 >= 64) & (s1 <= 95))
    c[:, 964] = (s1 <= 31)
    c[:, 965] = ((s1 >= 32) & (s1 <= 63))
    c[:, 966] = ((s1 >= 64) & (s1 <= 95))
    c[:, 967] = (s1 >= 96)
    c[:, 1024:1152] = (same & (sv <= tv))
    c[:, 1152:1280] = (same & (sv >= tv))
    return c


def rope_table():
    half = 8
    inv = np.exp(-np.arange(half, dtype=np.float32) * np.float32(2.0 / 16) * np.float32(np.log(500000.0))).astype(np.float32)
    ang = (np.arange(4096, dtype=np.float32)[:, None] * inv[None, :]).astype(np.float32)
    return np.concatenate([np.cos(ang), np.sin(ang)], axis=1).astype(np.float32)


def device_inputs(w):
    im = {}
    for k in ("norm_g", "ffn_w1", "ffn_w2", "hg_w_in", "hg_w_o", "hg_g_norm", "hg_lb", "sw_w_qkv", "sw_w_o",
              "sw_sinks", "na_w_qkv", "na_w_o"):
        im[k] = np.ascontiguousarray(np.asarray(w[k], dtype=np.float32))
    im["final_norm_g"] = np.ascontiguousarray(np.asarray(w["final_norm_g"], dtype=np.float32).reshape(1, D))
    im["na_bias"] = na_bias_layout(w["na_rpb"])
    im["consts"] = const_tables()
    im["rope"] = rope_table()
    return im


FULL_STAGES = [("ffn", 0, 0), ("hg", 0, 0), ("ffn", 0, 1),
               ("ffn", 1, 0), ("swa", 1), ("ffn", 1, 1),
               ("ffn", 2, 0), ("na", 2), ("ffn", 2, 1),
               ("ffn", 3, 0), ("hg", 3, 1), ("ffn", 3, 1, True)]
SEQS = [4096, 2048, 2048]
_NC_CACHE = {}


def kernel(x_prompt, x_sample, norm_g, final_norm_g, ffn_w1, ffn_w2, hg_w_in, hg_w_o, hg_g_norm, hg_lb,
           sw_w_qkv, sw_w_o, sw_sinks, na_w_qkv, na_w_o, na_rpb, _stages=None):
    stages = _stages or FULL_STAGES
    key = repr(stages)
    if key not in _NC_CACHE:
        _NC_CACHE[key] = build(SEQS, stages)
    nc = _NC_CACHE[key]
    w = dict(norm_g=norm_g, final_norm_g=final_norm_g, ffn_w1=ffn_w1, ffn_w2=ffn_w2, hg_w_in=hg_w_in,
             hg_w_o=hg_w_o, hg_g_norm=hg_g_norm, hg_lb=hg_lb, sw_w_qkv=sw_w_qkv, sw_w_o=sw_w_o,
             sw_sinks=sw_sinks, na_w_qkv=na_w_qkv, na_w_o=na_w_o, na_rpb=na_rpb)
    base = device_inputs(w)
    xp = np.asarray(x_prompt, dtype=np.float32)
    xs = np.asarray(x_sample, dtype=np.float32)
    in_maps = []
    for c in range(NCORES):
        im = dict(base)
        im["x"] = np.ascontiguousarray(np.concatenate([xp[c], xs[2 * c], xs[2 * c + 1]], axis=0))
        in_maps.append(im)
    res = run_bass_kernel_spmd(nc, in_maps, core_ids=list(range(NCORES)))
    yp = np.empty((8, 4096, D), np.float32)
    ys = np.empty((16, 2048, D), np.float32)
    for c in range(NCORES):
        y = np.asarray(res.results[c]["y"])
        yp[c] = y[0:4096]
        ys[2 * c] = y[4096:6144]
        ys[2 * c + 1] = y[6144:8192]
    return (yp, ys)
```
